# Optimizing a Trainium2 kernel written in Bass

```python
import jax
import jax.numpy as jnp
from jax import lax
import numpy as np


D_MODEL = 1024
BATCH = 8
SEQ = 4096
DEPTH = 4

CTX_LEN = 256
GRID_W = 64
N_BRANCH = 4
BRANCH_W = D_MODEL // 2
RET_HEADS = 4
RET_DH = BRANCH_W // RET_HEADS
RET_CHUNK = 128
SC_KERNEL = 3
CF_KERNEL = 31
ATT_HEADS = 8
ATT_KV_HEADS = 2
ATT_DH = BRANCH_W // ATT_HEADS
ATT_KV_W = ATT_KV_HEADS * ATT_DH
ATT_BLOCK = 128
WINDOW = 128
ROPE_BASE = 10000.0
ROPE_AXIS_FREQS = ATT_DH // 4
EPS = 1e-6
NEG_INF = -1e30

IN_NAMES = ('ret_q', 'ret_k', 'ret_v', 'ret_z',
            'sc_b', 'sc_c', 'sc_x', 'sc_z',
            'cf_glu', 'cf_z',
            'att_q', 'att_k', 'att_v', 'att_z',
            'merge_gates')
IN_SIZES = (BRANCH_W, BRANCH_W, BRANCH_W, BRANCH_W,
            BRANCH_W, BRANCH_W, BRANCH_W, BRANCH_W,
            2 * BRANCH_W, BRANCH_W,
            BRANCH_W, ATT_KV_W, ATT_KV_W, BRANCH_W,
            N_BRANCH * D_MODEL)
IN_TOTAL = sum(IN_SIZES)

kernel_name = 'hybrid_parallel_gated_dit_block'


def split_proj(y):
    offsets = [int(o) for o in np.cumsum(IN_SIZES)[:-1]]
    return dict(zip(IN_NAMES, jnp.split(y, offsets, axis=-1)))


def in_cols(w, name):
    i = IN_NAMES.index(name)
    start = sum(IN_SIZES[:i])
    return w[:, start:start + IN_SIZES[i]]


def heads(t, n):
    return t.reshape(t.shape[:-1] + (n, t.shape[-1] // n))


def rmsnorm(x, w):
    xf = x.astype(jnp.float32)
    y = xf * lax.rsqrt(jnp.mean(xf * xf, axis=-1, keepdims=True) + EPS)
    return (y * w).astype(x.dtype)


def layer_norm(u, w, b):
    uf = u.astype(jnp.float32)
    mu = jnp.mean(uf, axis=-1, keepdims=True)
    var = jnp.mean(jnp.square(uf - mu), axis=-1, keepdims=True)
    return (uf - mu) * lax.rsqrt(var + EPS) * w + b


def head_group_norm(o, w):
    of = o.astype(jnp.float32)
    mu = jnp.mean(of, axis=-1, keepdims=True)
    var = jnp.mean(jnp.square(of - mu), axis=-1, keepdims=True)
    y = (of - mu) * lax.rsqrt(var + EPS)
    return y.reshape(o.shape[:-2] + (-1,)) * w


def ada_modulation(cvec, w, b):
    m = jax.nn.silu(cvec) @ w + b
    return jnp.split(m, 3, axis=-1)


def modulate(h, shift, scale):
    return h * (1 + scale) + shift


def depthwise_conv(u, w):
    K, C = w.shape
    return lax.conv_general_dilated(
        u, w[:, None, :].astype(u.dtype), window_strides=(1,), padding=[(K // 2, K // 2)],
        dimension_numbers=('NWC', 'WIO', 'NWC'), feature_group_count=C)


def axial_rope_tables(row, col):
    inv = ROPE_BASE ** (-jnp.arange(ROPE_AXIS_FREQS, dtype=jnp.float32) / ROPE_AXIS_FREQS)
    ang = jnp.concatenate([row.astype(jnp.float32)[:, None] * inv[None],
                           col.astype(jnp.float32)[:, None] * inv[None]], axis=-1)
    return jnp.cos(ang)[:, None, :], jnp.sin(ang)[:, None, :]


def axial_rope(t, cos, sin):
    half = t.shape[-1] // 2
    t1 = t[..., :half].astype(jnp.float32)
    t2 = t[..., half:].astype(jnp.float32)
    return jnp.concatenate([t1 * cos - t2 * sin, t2 * cos + t1 * sin], axis=-1).astype(t.dtype)


def retention_dir(q, k, v, log_gamma, s0, strict):
    B, N, H, d = q.shape
    C = RET_CHUNK
    nc = N // C
    qc, kc, vc = (t.reshape(B, nc, C, H, d) for t in (q, k, v))
    pos = jnp.arange(C)
    diff = pos[:, None] - pos[None, :]
    keep = (diff > 0) if strict else (diff >= 0)
    expo = jnp.where(keep, diff, 0).astype(jnp.float32)
    decay = jnp.where(keep[None], jnp.exp(expo[None] * log_gamma[:, None, None]), 0.0)
    scores = jnp.einsum('bnihd,bnjhd->bnhij', qc, kc) * decay
    intra = jnp.einsum('bnhij,bnjhe->bnihe', scores, vc)
    posf = pos.astype(jnp.float32)
    k_w = jnp.exp((C - 1 - posf)[:, None] * log_gamma[None])
    q_w = jnp.exp((posf + 1)[:, None] * log_gamma[None])
    kv = jnp.einsum('bnjhd,jh,bnjhe->nbhde', kc, k_w, vc)
    chunk_decay = jnp.exp(C * log_gamma)[None, :, None, None]

    def step(s, kv_c):
        return s * chunk_decay + kv_c, s

    s_final, s_prev = lax.scan(step, s0, kv)
    cross = jnp.einsum('bnihd,ih,nbhde->bnihe', qc, q_w, s_prev)
    return (intra + cross).reshape(B, N, H, d), s_final


def retention_bidir(q, k, v, log_gamma, s0_f, s0_b):
    out_f, s_f = retention_dir(q, k, v, log_gamma[0], s0_f, False)
    flip = lambda t: jnp.flip(t, axis=1)
    out_b, s_b = retention_dir(flip(q), flip(k), flip(v), log_gamma[1], s0_b, True)
    return out_f + flip(out_b), s_f, s_b


def retention_context_states(k, v, log_gamma):
    L = k.shape[1]
    pos = jnp.arange(L, dtype=jnp.float32)
    w_f = jnp.exp((L - 1 - pos)[:, None] * log_gamma[0][None])
    w_b = jnp.exp(pos[:, None] * log_gamma[1][None])
    s_f = jnp.einsum('blhd,lh,blhe->bhde', k, w_f, v)
    s_b = jnp.einsum('blhd,lh,blhe->bhde', k, w_b, v)
    return s_f, s_b


def windowed_gqa(q, k, v, k_ctx, v_ctx, sink):
    B, S, H, d = q.shape
    G = H // ATT_KV_HEADS
    nb = S // ATT_BLOCK
    L = k_ctx.shape[1]
    qb = q.reshape(B, nb, ATT_BLOCK, ATT_KV_HEADS, G, d)

    def band(t):
        tp = jnp.pad(t, ((0, 0), (ATT_BLOCK, ATT_BLOCK), (0, 0), (0, 0)))
        tp = tp.reshape(B, nb + 2, ATT_BLOCK, ATT_KV_HEADS, d)
        return jnp.concatenate([tp[:, :-2], tp[:, 1:-1], tp[:, 2:]], axis=2)

    kw, vw = band(k), band(v)
    qi = jnp.arange(ATT_BLOCK)
    kj = jnp.arange(3 * ATT_BLOCK) - ATT_BLOCK
    in_band = jnp.abs(kj[None, :] - qi[:, None]) <= WINDOW
    key_pos = jnp.arange(nb)[:, None] * ATT_BLOCK + kj[None, :]
    in_seq = (key_pos >= 0) & (key_pos < S)
    mask = in_band[None] & in_seq[:, None, :]
    scale = d ** -0.5
    s_loc = jnp.einsum('bnqhgd,bnkhd->bnhgqk', qb, kw).astype(jnp.float32) * scale
    s_loc = jnp.where(mask[None, :, None, None], s_loc, NEG_INF)
    s_ctx = jnp.einsum('bnqhgd,bchd->bnhgqc', qb, k_ctx).astype(jnp.float32) * scale
    s_sink = jnp.broadcast_to(sink.astype(jnp.float32).reshape(ATT_KV_HEADS, G, 1, 1),
                              s_loc.shape[:-1] + (1,))
    p = jax.nn.softmax(jnp.concatenate([s_loc, s_ctx, s_sink], axis=-1), axis=-1)
    nk = 3 * ATT_BLOCK
    p_loc = p[..., :nk].astype(v.dtype)
    p_ctx = p[..., nk:nk + L].astype(v.dtype)
    o = (jnp.einsum('bnhgqk,bnkhd->bnqhgd', p_loc, vw)
         + jnp.einsum('bnhgqc,bchd->bnqhgd', p_ctx, v_ctx))
    return o.reshape(B, S, H * d)


def context_attention(q, k, v, sink):
    B, L, H, d = q.shape
    G = H // ATT_KV_HEADS
    qg = q.reshape(B, L, ATT_KV_HEADS, G, d)
    s = jnp.einsum('bqhgd,bkhd->bhgqk', qg, k).astype(jnp.float32) * d ** -0.5
    s_sink = jnp.broadcast_to(sink.astype(jnp.float32).reshape(ATT_KV_HEADS, G, 1, 1), s.shape[:-1] + (1,))
    p = jax.nn.softmax(jnp.concatenate([s, s_sink], axis=-1), axis=-1)[..., :L]
    o = jnp.einsum('bhgqk,bkhd->bqhgd', p.astype(v.dtype), v)
    return o.reshape(B, L, H * d)


def short_conv(b_gate, c_gate, xv, w):
    return b_gate * depthwise_conv(c_gate * xv, w)


def conformer_conv(glu_in, w, bias, ln_w, ln_b):
    a, g = jnp.split(glu_in, 2, axis=-1)
    u = depthwise_conv(a * jax.nn.sigmoid(g), w) + bias
    return jax.nn.silu(layer_norm(u, ln_w, ln_b)).astype(glu_in.dtype)


def mixer_output(p, ret_o, att_o, ret_gn_w, sc_conv_w, cf_conv_w, cf_conv_b, cf_ln_w, cf_ln_b,
                 w_branch, w_out):
    dt = p['ret_z'].dtype
    r = (head_group_norm(ret_o, ret_gn_w) * jax.nn.silu(p['ret_z'])).astype(dt)
    s = short_conv(p['sc_b'], p['sc_c'], p['sc_x'], sc_conv_w) * jax.nn.silu(p['sc_z'])
    f = conformer_conv(p['cf_glu'], cf_conv_w, cf_conv_b, cf_ln_w, cf_ln_b) * jax.nn.silu(p['cf_z'])
    a = att_o * jax.nn.silu(p['att_z'])
    gates = jnp.split(p['merge_gates'], N_BRANCH, axis=-1)
    merged = sum(jax.nn.sigmoid(g) * (br @ w_branch[i])
                 for i, (g, br) in enumerate(zip(gates, (r, s, f, a))))
    return merged @ w_out


def setup_inputs(seed: int = 0) -> dict:
    key = jax.random.key(seed)
    ks = jax.random.split(key, 20)
    f32 = jnp.float32
    D = D_MODEL

    def nrm(k, shape, fan_in):
        return jax.random.normal(k, shape, f32) * fan_in ** -0.5

    decay_logit = jnp.log(2.0 ** (5.0 + jnp.arange(RET_HEADS, dtype=f32)) - 1.0)
    return {
        'x': jax.random.normal(ks[0], (BATCH, SEQ, D), f32),
        'c': jax.random.normal(ks[1], (BATCH, D), f32),
        'ctx': jax.random.normal(ks[2], (BATCH, CTX_LEN, D), f32),
        'c_ctx': jax.random.normal(ks[3], (D,), f32),
        'w_mod': nrm(ks[4], (DEPTH, D, 3 * D), D),
        'b_mod': 0.02 * jax.random.normal(ks[5], (DEPTH, 3 * D), f32),
        'norm_w': 1.0 + 0.1 * jax.random.normal(ks[6], (DEPTH, D), f32),
        'w_in': nrm(ks[7], (DEPTH, D, IN_TOTAL), D),
        'ret_decay': decay_logit[None, None, :] + 0.1 * jax.random.normal(ks[8], (DEPTH, 2, RET_HEADS), f32),
        'ret_gn_w': 1.0 + 0.1 * jax.random.normal(ks[9], (DEPTH, BRANCH_W), f32),
        'sc_conv_w': nrm(ks[10], (DEPTH, SC_KERNEL, BRANCH_W), SC_KERNEL),
        'cf_conv_w': nrm(ks[11], (DEPTH, CF_KERNEL, BRANCH_W), CF_KERNEL),
        'cf_conv_b': 0.02 * jax.random.normal(ks[12], (DEPTH, BRANCH_W), f32),
        'cf_ln_w': 1.0 + 0.1 * jax.random.normal(ks[13], (DEPTH, BRANCH_W), f32),
        'cf_ln_b': 0.02 * jax.random.normal(ks[14], (DEPTH, BRANCH_W), f32),
        'att_sink': jax.random.normal(ks[15], (DEPTH, ATT_HEADS), f32),
        'w_branch': nrm(ks[16], (DEPTH, N_BRANCH, BRANCH_W, D), BRANCH_W),
        'w_out': nrm(ks[17], (DEPTH, D, D), D),
        'final_norm_w': 1.0 + 0.1 * jax.random.normal(ks[18], (D,), f32),
    }


def reference(x, c, ctx, c_ctx, w_mod, b_mod, norm_w, w_in, ret_decay, ret_gn_w, sc_conv_w,
              cf_conv_w, cf_conv_b, cf_ln_w, cf_ln_b, att_sink, w_branch, w_out, final_norm_w):
    B, S, _ = x.shape
    ROWS = S // GRID_W
    row = jnp.repeat(jnp.arange(ROWS), GRID_W)
    col = jnp.tile(jnp.arange(GRID_W), ROWS)
    cos, sin = axial_rope_tables(row, col)
    k_scale = RET_DH ** -0.5
    xc = ctx
    for l in range(DEPTH):
        ctx_out_needed = l < DEPTH - 1
        log_gamma = jax.nn.log_sigmoid(ret_decay[l].astype(jnp.float32))
        shift_x, scale_x, gate_x = ada_modulation(c, w_mod[l], b_mod[l])
        shift_c, scale_c, gate_c = ada_modulation(c_ctx, w_mod[l], b_mod[l])
        hx = modulate(rmsnorm(x, norm_w[l]), shift_x[:, None], scale_x[:, None])
        hc = modulate(rmsnorm(xc, norm_w[l]), shift_c, scale_c)
        px = split_proj(hx @ w_in[l])
        if ctx_out_needed:
            pc = split_proj(hc @ w_in[l])
        else:
            pc = {n: hc @ in_cols(w_in[l], n) for n in ('ret_k', 'ret_v', 'att_k', 'att_v')}
        rk_c = heads(pc['ret_k'], RET_HEADS) * k_scale
        rv_c = heads(pc['ret_v'], RET_HEADS)
        ak_c = heads(pc['att_k'], ATT_KV_HEADS)
        av_c = heads(pc['att_v'], ATT_KV_HEADS)
        if ctx_out_needed:
            zero = jnp.zeros((B, RET_HEADS, RET_DH, RET_DH), jnp.float32)
            ret_c, s_f, s_b = retention_bidir(heads(pc['ret_q'], RET_HEADS), rk_c, rv_c, log_gamma, zero, zero)
            att_c = context_attention(heads(pc['att_q'], ATT_HEADS), ak_c, av_c, att_sink[l])
            out_c = mixer_output(pc, ret_c, att_c, ret_gn_w[l], sc_conv_w[l], cf_conv_w[l], cf_conv_b[l],
                                 cf_ln_w[l], cf_ln_b[l], w_branch[l], w_out[l])
        else:
            s_f, s_b = retention_context_states(rk_c, rv_c, log_gamma)
        ret_x, _, _ = retention_bidir(heads(px['ret_q'], RET_HEADS), heads(px['ret_k'], RET_HEADS) * k_scale,
                                      heads(px['ret_v'], RET_HEADS), log_gamma, s_f, s_b)
        att_x = windowed_gqa(axial_rope(heads(px['att_q'], ATT_HEADS), cos, sin),
                             axial_rope(heads(px['att_k'], ATT_KV_HEADS), cos, sin),
                             heads(px['att_v'], ATT_KV_HEADS), ak_c, av_c, att_sink[l])
        out_x = mixer_output(px, ret_x, att_x, ret_gn_w[l], sc_conv_w[l], cf_conv_w[l], cf_conv_b[l],
                             cf_ln_w[l], cf_ln_b[l], w_branch[l], w_out[l])
        x = x + gate_x[:, None] * out_x
        if ctx_out_needed:
            xc = xc + gate_c * out_c
    return rmsnorm(x, final_norm_w)
```

```python
import numpy as np, contextlib
import concourse.bass as bass
import concourse.mybir as mybir

F32 = mybir.dt.float32
BF16 = mybir.dt.bfloat16
AF = mybir.ActivationFunctionType
ALU = mybir.AluOpType
AX = mybir.AxisListType
_ISZ = {F32: 4, BF16: 2}


def _norm_idx(idx, shape):
    if not isinstance(idx, tuple):
        idx = (idx,)
    idx = list(idx) + [slice(None)] * (len(shape) - len(idx))
    out = []
    for i, n in zip(idx, shape):
        if isinstance(i, int):
            out.append((i, i + 1, True))
        else:
            lo = 0 if i.start is None else i.start
            hi = n if i.stop is None else i.stop
            assert i.step in (None, 1) and 0 <= lo < hi <= n, (i, n)
            out.append((lo, hi, False))
    return out


class Mem:
    def __init__(self, name, birth=None, excl=False):
        self.name = name
        self.excl = excl
        self.recs = []
        self.birth = dict(birth or {})

    def summary(self):
        d = dict(self.birth)
        for _, w, r in self.recs:
            _merge(d, w)
            _merge(d, r)
        return d


def _merge(d, s):
    for k, v in s.items():
        if d.get(k, 0) < v:
            d[k] = v


def _ovl(a, b):
    for (al, ah), (bl, bh) in zip(a, b):
        if al >= bh or bl >= ah:
            return False
    return True


def _covers(a, b):
    for (al, ah), (bl, bh) in zip(a, b):
        if al > bl or ah < bh:
            return False
    return True


class V:
    __slots__ = ("ap", "mem", "box", "shape", "base")

    def __init__(self, ap, mem, box, shape, base=None):
        self.ap = ap
        self.mem = mem
        self.box = box
        self.shape = shape
        self.base = base

    def __getitem__(self, idx):
        if self.base is None:
            return V(self.ap[idx], self.mem, self.box, None, None)
        n = _norm_idx(idx, self.shape)
        box = list(self.box)
        newbase = []
        newshape = []
        for (lo, hi, drop), (md, off) in zip(n, self.base):
            box[md] = (off + lo, off + hi)
            if not drop:
                newbase.append((md, off + lo))
                newshape.append(hi - lo)
        return V(self.ap[idx], self.mem, tuple(box), tuple(newshape), newbase)

    def re(self, ap):
        return V(ap, self.mem, self.box, tuple(ap.shape), None)


def mkview(ap, mem, shape):
    return V(ap, mem, tuple((0, s) for s in shape), tuple(shape), [(i, 0) for i in range(len(shape))])


class Sched:
    SAME_ENGINE_SYNC = True
    EPOCH = 30000
    NQ = 8

    def __init__(self, nc, es):
        self.nc = nc
        self.es = es
        self.E = {"pe": nc.tensor, "dve": nc.vector, "act": nc.scalar, "pool": nc.gpsimd, "sp": nc.sync}
        self.semobj = {}
        self.nsem = 0
        self.cur = {}
        self.cnt = {}
        self.mysems = {e: set() for e in self.E}
        for e in self.E:
            self._new_epoch(e)
        self.waited = {e: {} for e in self.E}
        self.dq = {}
        for q in ("sp", "pool", "act"):
            sems = [self._sem(f"dq_{q}{i}") for i in range(self.NQ)]
            self.dq[q] = {"sems": sems, "i": 0}
        self.nops = 0
        self.nwaits = 0

    def _sem(self, name):
        s = self.es.enter_context(self.nc.semaphore(name))
        sid = self.nsem
        self.nsem += 1
        self.semobj[sid] = s
        return sid

    def _new_epoch(self, e):
        sid = self._sem(f"e_{e}{self.nsem}")
        self.cur[e] = sid
        self.cnt[e] = 0
        self.mysems[e].add(sid)

    def _deps(self, reads, writes):
        d = {}
        for v in reads:
            m = v.mem
            _merge(d, m.birth)
            for box, w, r in m.recs:
                if _ovl(box, v.box):
                    _merge(d, w)
                    if m.excl:
                        _merge(d, r)
        for v in writes:
            m = v.mem
            _merge(d, m.birth)
            for box, w, r in m.recs:
                if _ovl(box, v.box):
                    _merge(d, w)
                    _merge(d, r)
        return d

    def _record(self, reads, writes, tok):
        sid, val = tok
        for v in reads:
            for rec in v.mem.recs:
                if _ovl(rec[0], v.box):
                    if rec[2].get(sid, 0) < val:
                        rec[2][sid] = val
        for v in writes:
            m = v.mem
            keep = []
            carry_r = {}
            for rec in m.recs:
                if _covers(v.box, rec[0]):
                    continue
                keep.append(rec)
            keep.append([v.box, {sid: val}, {}])
            m.recs = keep

    def _emit_waits(self, eng, deps):
        w = self.waited[eng]
        E = self.E[eng]
        for sid, val in deps.items():
            if sid in self.mysems[eng]:
                if eng == "pe" or eng == "sp" or not self.SAME_ENGINE_SYNC:
                    continue
            if w.get(sid, 0) < val:
                E.wait_ge(self.semobj[sid], val)
                w[sid] = val
                self.nwaits += 1

    def issue(self, eng, fn, reads, writes, signal=True):
        deps = self._deps(reads, writes)
        self._emit_waits(eng, deps)
        inst = fn()
        self.nops += 1
        sid = self.cur[eng]
        if signal:
            self.cnt[eng] += 1
            inst.then_inc(self.semobj[sid], 1)
            tok = (sid, self.cnt[eng])
        else:
            tok = (sid, self.cnt[eng] + 1)
        self._record(reads, writes, tok)
        if signal and self.cnt[eng] >= self.EPOCH:
            self._new_epoch(eng)
        return tok

    def dma(self, q, out, in_, **kw):
        deps = self._deps([in_], [out])
        D = self.dq[q]
        i = D["i"]
        D["i"] += 1
        sid = D["sems"][i % self.NQ]
        rnd = i // self.NQ
        if rnd > 0:
            deps_prev = {sid: 16 * rnd}
            _merge(deps, deps_prev)
        self._emit_waits(q, deps)
        inst = self.E[q].dma_start(out=out.ap, in_=in_.ap, **kw)
        inst.then_inc(self.semobj[sid], 16)
        tok = (sid, 16 * (rnd + 1))
        self._record([in_], [out], tok)
        self.nops += 1
        return tok

    def wait_all(self, eng, mems):
        d = {}
        for m in mems:
            _merge(d, m.summary())
        self._emit_waits(eng, d)

    def mm(self, out, lhsT, rhs, start=True, stop=True, signal=None, **kw):
        if signal is None:
            signal = stop
        return self.issue("pe", lambda: self.nc.tensor.matmul(out.ap, lhsT=lhsT.ap, rhs=rhs.ap, start=start, stop=stop, **kw),
                          [lhsT, rhs], [out], signal=signal)

    def transpose(self, out, in_, ident, signal=True):
        return self.issue("pe", lambda: self.nc.tensor.transpose(out=out.ap, in_=in_.ap, identity=ident.ap),
                          [in_, ident], [out], signal=signal)

    def act(self, out, in_, func, bias=None, scale=None, eng="act"):
        reads = [in_]
        kw = {}
        if bias is not None:
            if isinstance(bias, V):
                reads.append(bias); kw["bias"] = bias.ap
            else:
                kw["bias"] = float(bias)
        if scale is not None:
            if isinstance(scale, V):
                reads.append(scale); kw["scale"] = scale.ap
            else:
                kw["scale"] = float(scale)
        return self.issue("act", lambda: self.nc.scalar.activation(out=out.ap, in_=in_.ap, func=func, **kw), reads, [out])

    def tt(self, eng, out, in0, in1, op):
        return self.issue(eng, lambda: self.E[eng].tensor_tensor(out=out.ap, in0=in0.ap, in1=in1.ap, op=op), [in0, in1], [out])

    def ts(self, eng, out, in0, s1, s2, op0, op1=None):
        reads = [in0]
        a1 = s1.ap if isinstance(s1, V) else s1
        a2 = s2.ap if isinstance(s2, V) else s2
        if isinstance(s1, V): reads.append(s1)
        if isinstance(s2, V): reads.append(s2)
        if op1 is None:
            f = lambda: self.E[eng].tensor_scalar(out=out.ap, in0=in0.ap, scalar1=a1, scalar2=None, op0=op0)
        else:
            f = lambda: self.E[eng].tensor_scalar(out=out.ap, in0=in0.ap, scalar1=a1, scalar2=a2, op0=op0, op1=op1)
        return self.issue(eng, f, reads, [out])

    def stt(self, eng, out, in0, scalar, in1, op0, op1):
        reads = [in0, in1]
        a = scalar.ap if isinstance(scalar, V) else scalar
        if isinstance(scalar, V): reads.append(scalar)
        return self.issue(eng, lambda: self.E[eng].scalar_tensor_tensor(out=out.ap, in0=in0.ap, scalar=a, in1=in1.ap, op0=op0, op1=op1),
                          reads, [out])

    def copy(self, eng, out, in_):
        if eng == "act":
            return self.issue("act", lambda: self.nc.scalar.copy(out=out.ap, in_=in_.ap), [in_], [out])
        return self.issue(eng, lambda: self.E[eng].tensor_copy(out=out.ap, in_=in_.ap), [in_], [out])

    def memset(self, eng, out, val):
        return self.issue(eng, lambda: self.E[eng].memset(out.ap, val), [], [out])

    def reduce(self, eng, out, in_, op, axis=AX.X):
        return self.issue(eng, lambda: self.E[eng].tensor_reduce(out=out.ap, in_=in_.ap, axis=axis, op=op), [in_], [out])

    def recip(self, out, in_):
        return self.issue("dve", lambda: self.nc.vector.reciprocal(out=out.ap, in_=in_.ap), [in_], [out])


class Arena:
    def __init__(self, sched, ap_f32, nwords):
        self.s = sched
        self.ap = ap_f32
        self.n = nwords
        self.top = 0
        self.dead = []
        self.live = []

    def alloc(self, name, shape, dtype, p=128):
        isz = _ISZ[dtype]
        free = int(np.prod(shape[1:]))
        nbytes = free * isz
        nw = (nbytes + 3) // 4
        lo, hi = self.top, self.top + nw
        assert hi <= self.n, f"arena overflow allocating {name}: {hi} > {self.n}"
        self.top = hi
        birth = {}
        nd = []
        for (dl, dh, toks) in self.dead:
            if dl < hi and lo < dh:
                _merge(birth, toks)
                if lo <= dl and dh <= hi:
                    continue
            nd.append((dl, dh, toks))
        self.dead = nd
        mem = Mem(name, birth)
        ap = self.ap[0:shape[0], lo:hi]
        if dtype != F32:
            ap = ap.bitcast(dtype)
        ap = ap[:, 0:free]
        if len(shape) > 2:
            names = " ".join(f"d{i}" for i in range(1, len(shape)))
            kw = {f"d{i}": shape[i] for i in range(2, len(shape))}
            ap = ap.rearrange(f"p ({names}) -> p {names}", **kw)
        self.live.append((lo, hi, mem))
        return mkview(ap, mem, shape)

    def mark(self):
        return (self.top, len(self.live))

    def release(self, mark):
        top, nl = mark
        while len(self.live) > nl:
            lo, hi, mem = self.live.pop()
            self.dead.append((lo, hi, mem.summary()))
        self.top = top

    @contextlib.contextmanager
    def scope(self):
        m = self.mark()
        yield
        self.release(m)


D = 1024; BW = 512; KS_RET = 128 ** -0.5; ATT_SCALE = 64 ** -0.5; EPS = 1e-6
IN_OFF = {}
_o = 0
for _n, _s in zip(('ret_q','ret_k','ret_v','ret_z','sc_b','sc_c','sc_x','sc_z','cf_a','cf_g','cf_z','att_q','att_k','att_v','att_z','gates'),
                  (512,512,512,512,512,512,512,512,512,512,512,512,128,128,512,4096)):
    IN_OFF[_n] = _o; _o += _s
IN_TOTAL = _o
NGRAN = 23
G_RQ, G_RK, G_RV, G_RZ, G_SB, G_SC, G_SX, G_SZ, G_CA, G_CG, G_CZ, G_AQ, G_AQS, G_AKV, G_AZ, G_GATE = 0,1,2,3,4,5,6,7,8,9,10,11,12,13,14,15


def par_layout(DEPTH):
    lay = {}; off = 0
    for name, shp in (("b_mod", (DEPTH, 24)), ("norm_w", (DEPTH, 8)), ("fnw", (8,)), ("gn_w", (DEPTH, 4)), ("scw", (DEPTH, 4, 3)),
                      ("cfw", (DEPTH, 4, 31)), ("cfb", (DEPTH, 4)), ("lnw", (DEPTH, 4)), ("lnb", (DEPTH, 4)),
                      ("rdec", (DEPTH, 8)), ("sink", (DEPTH, 8)), ("cvec", (8, 2))):
        n = int(np.prod(shp)); lay[name] = (off, shp); off += n
    return lay, off

CON_NAMES = ("ident", "posdF", "posdB", "MF", "MB", "rampF1", "rampB", "c128", "onesf", "mprev", "mnext")

def con_layout():
    lay = {}; off = 0
    for nm in CON_NAMES:
        lay[nm] = off; off += 128
    lay["rp127"] = off; off += 1
    lay["rp"] = off; off += 1
    return lay, off


def host_consts(S, L):
    lay, n = con_layout()
    C = np.zeros((128, n), np.float32)
    j = np.arange(128)[:, None].astype(np.float32); i = np.arange(128)[None, :].astype(np.float32)
    def put(nm, a): C[:, lay[nm]:lay[nm] + 128] = a
    put("ident", np.eye(128)); put("posdF", np.maximum(i - j, 0)); put("posdB", np.maximum(j - i, 0))
    put("MF", (i >= j)); put("MB", (j > i)); put("rampF1", np.broadcast_to(i + 1, (128, 128))); put("rampB", np.broadcast_to(128 - i, (128, 128)))
    put("c128", np.full((128, 128), 128.0)); put("onesf", np.ones((128, 128)))
    put("mprev", (j >= i)); put("mnext", (j <= i))
    C[:, lay["rp127"]] = 127 - np.arange(128); C[:, lay["rp"]] = np.arange(128)
    NT = L + S
    t = np.arange(S); row = (t // 64).astype(np.float32); col = (t % 64).astype(np.float32)
    inv = (10000.0 ** (-np.arange(16, dtype=np.float32) / 16)).astype(np.float32)
    ang = np.concatenate([row[:, None] * inv[None], col[:, None] * inv[None]], -1).astype(np.float32)
    cos = np.cos(ang).astype(np.float32); sin = np.sin(ang).astype(np.float32)
    R = np.zeros((2, 128, NT), np.float32); R[0, :, :L] = 1.0
    for p in range(128):
        d = p % 64
        R[0, p, L:] = cos[:, d % 32]
        R[1, p, L:] = (-sin[:, d % 32]) if d < 32 else sin[:, d % 32]
    return C, R


def host_params(inp, b, DEPTH):
    lay, n = par_layout(DEPTH)
    P = np.zeros((128, n), np.float32)
    def put(nm, arr):
        off, shp = lay[nm]; P[:, off:off + int(np.prod(shp))] = arr.reshape(128, -1)
    fm = lambda a, k: np.ascontiguousarray(a.reshape(a.shape[:-1] + (k, 128)))
    put("b_mod", np.moveaxis(fm(inp["b_mod"][:DEPTH], 24), -1, 0))
    put("norm_w", np.moveaxis(fm(inp["norm_w"][:DEPTH], 8), -1, 0))
    put("fnw", np.moveaxis(fm(inp["final_norm_w"], 8), -1, 0))
    put("gn_w", np.moveaxis(fm(inp["ret_gn_w"][:DEPTH], 4), -1, 0))
    put("scw", np.transpose(inp["sc_conv_w"][:DEPTH].reshape(DEPTH, 3, 4, 128), (3, 0, 2, 1)))
    put("cfw", np.transpose(inp["cf_conv_w"][:DEPTH].reshape(DEPTH, 31, 4, 128), (3, 0, 2, 1)))
    put("cfb", np.moveaxis(fm(inp["cf_conv_b"][:DEPTH], 4), -1, 0))
    put("lnw", np.moveaxis(fm(inp["cf_ln_w"][:DEPTH], 4), -1, 0))
    put("lnb", np.moveaxis(fm(inp["cf_ln_b"][:DEPTH], 4), -1, 0))
    put("rdec", np.broadcast_to(inp["ret_decay"][:DEPTH].reshape(1, DEPTH, 8), (128, DEPTH, 8)))
    put("sink", np.broadcast_to(inp["att_sink"][:DEPTH].reshape(1, DEPTH, 8), (128, DEPTH, 8)))
    cv = np.stack([inp["c"][b].reshape(8, 128), inp["c_ctx"].reshape(8, 128)], -1)
    put("cvec", np.transpose(cv, (1, 0, 2)))
    return P


class StopBuild(Exception):
    pass


STQ = "act"


def build_program(S, L, DEPTH, TB=512, STOP=None, DEBUG=False):
    def stage(k):
        if STOP is not None and k == STOP:
            raise StopBuild()
    NT = L + S; NCH = NT // 128; NCC = L // 128
    NTP = NT + 64
    pidx = lambda g: g + 16 if g < L else g + 48
    blocks = [(0, L, True)] + [(L + i * TB, TB, False) for i in range(S // TB)]
    playout, NPAR = par_layout(DEPTH); clayout, NCON = con_layout()
    nc = bass.Bass("TRN2", target_bir_lowering=False)
    dt = nc.dram_tensor
    xT_d = dt("xT", [D, S], F32, kind="ExternalInput"); cT_d = dt("cT", [D, L], F32, kind="ExternalInput")
    par_d = dt("par", [128, NPAR], F32, kind="ExternalInput"); con_d = dt("con", [128, NCON], F32, kind="ExternalInput")
    rope_d = dt("rope", [2, 128, NT], F32, kind="ExternalInput")
    wmod_d = dt("w_mod", [DEPTH, D, 3 * D], F32, kind="ExternalInput")
    win_d = dt("w_in", [DEPTH, D, IN_TOTAL], F32, kind="ExternalInput")
    wbr_d = dt("w_branch", [DEPTH, 4, BW, D], F32, kind="ExternalInput")
    wout_d = dt("w_out", [DEPTH, D, D], F32, kind="ExternalInput")
    yT_d = dt("yT", [D, S], F32, kind="ExternalOutput")
    XS_d = [dt(f"XS{i}", [D, NT], F32, kind="Internal") for i in range(2)]
    WS_d = [dt(f"WS{l}", [D, NGRAN * 512], BF16, kind="Internal") for l in range(DEPTH)]
    WB_d = [dt(f"WB{l}", [4, BW, D], BF16, kind="Internal") for l in range(DEPTH)]
    WO_d = [dt(f"WO{l}", [D, D], BF16, kind="Internal") for l in range(DEPTH)]
    CD_d = [dt(f"CD{l}", [5, 128, 31 * 128], BF16, kind="Internal") for l in range(DEPTH)]
    dk = "ExternalOutput" if DEBUG else "Internal"
    SF_d = dt("SFs", [NCH, 128, 512], BF16, kind=dk); SB_d = dt("SBs", [NCH, 128, 512], BF16, kind=dk)
    USC_d = dt("USC", [BW, NTP], BF16, kind=dk); UCF_d = dt("UCF", [BW, NTP], BF16, kind=dk)
    DBG_d = dt("DBG", [4, BW, NT], BF16, kind=dk) if DEBUG else None

    es = contextlib.ExitStack()
    with es:
        es.push(lambda et, ev, tb: et is not None and issubclass(et, StopBuild))
        es.enter_context(nc.allow_low_precision("bf16 matmul operands, fp32 accumulation"))
        es.enter_context(nc.allow_non_contiguous_dma("small strided weight gathers"))
        S_ = Sched(nc, es)
        NW = 52800
        ar_t = es.enter_context(nc.sbuf_tensor("arena", [128, NW], F32))
        AR = Arena(S_, ar_t[:, :], NW)
        psb = [es.enter_context(nc.psum_tensor(f"psb{i}", [128, 512], F32)) for i in range(8)]
        psm = [Mem(f"ps{i}", excl=True) for i in range(8)]
        pstate = {"i": 0}

        def pv(bank, dtype=F32, shape=(128, 512)):
            n = int(np.prod(shape[1:]))
            if dtype == F32:
                ap = psb[bank][:, 0:n]
            else:
                ap = psb[bank][:, 0:(n + 1) // 2].bitcast(BF16)[:, 0:n]
            if len(shape) > 2:
                names = " ".join(f"d{i}" for i in range(1, len(shape)))
                kw = {f"d{i}": shape[i] for i in range(2, len(shape))}
                ap = ap.rearrange(f"p ({names}) -> p {names}", **kw)
            return V(ap, psm[bank], ((0, 1),), None, None)

        def ps(dtype=F32, shape=(128, 512)):
            b = pstate["i"] % 6; pstate["i"] += 1
            return pv(b, dtype, shape)

        def dview(t, ap, box):
            return V(ap, dmem[t.name], box, None, None)
        dmem = {}
        for t in [xT_d, cT_d, par_d, con_d, rope_d, wmod_d, win_d, wbr_d, wout_d, yT_d, SF_d, SB_d, USC_d, UCF_d] + XS_d + WS_d + WB_d + WO_d + CD_d:
            dmem[t.name] = Mem(t.name)

        PAR = AR.alloc("PAR", (128, NPAR), F32); CON = AR.alloc("CON", (128, NCON), F32)
        S_.dma("sp", PAR, dview(par_d, par_d.ap(), ((0, 128), (0, NPAR))))
        S_.dma("sp", CON, dview(con_d, con_d.ap(), ((0, 128), (0, NCON))))
        def par(nm, *idx):
            off, shp = playout[nm]; n = int(np.prod(shp))
            v = PAR[:, off:off + n]
            if len(shp) > 1:
                names = " ".join(f"d{i}" for i in range(len(shp)))
                kw = {f"d{i}": shp[i] for i in range(1, len(shp))}
                v = v.re(v.ap.rearrange(f"p ({names}) -> p {names}", **kw))
            return v[(slice(None),) + idx] if idx else v
        def con(nm, w=128):
            o = clayout[nm]; return CON[:, o:o + w]
        HX = AR.alloc("HX", (128, 8, NT), BF16)
        KR = AR.alloc("KR", (128, NT), BF16)
        VA = AR.alloc("VA", (128, NCH, 2, 65), BF16)
        QA = AR.alloc("QA", (128, 4, TB), BF16); QB = AR.alloc("QB", (128, 4, TB), BF16)
        NWR = 4
        WR = [AR.alloc(f"WR{i}", (128, 4096), BF16) for i in range(NWR)]
        wst = {"i": 0}
        IDB = AR.alloc("IDB", (128, 128), BF16); ONESD = AR.alloc("ONESD", (128, 128), BF16); ONESC = AR.alloc("ONESC", (128, 128), BF16)
        MPREV = AR.alloc("MPREV", (128, 128), BF16); MNEXT = AR.alloc("MNEXT", (128, 128), BF16)
        MOD = AR.alloc("MOD", (128, DEPTH, 24, 2), F32)
        AXC = AR.alloc("AXC", (128, DEPTH, 2, 8), F32)
        LG = AR.alloc("LG", (128, 8), F32); ESINK = AR.alloc("ESINK", (128, 8), F32)
        DT = AR.alloc("DT", (128, 4, 128), F32); QWF = AR.alloc("QWF", (128, 4, 128), F32); QWB = AR.alloc("QWB", (128, 4, 128), F32)
        KWF = AR.alloc("KWF", (128, 512), F32); KWB = AR.alloc("KWB", (128, 512), F32)
        GF = AR.alloc("GF", (128, 512), F32); GB = AR.alloc("GB", (128, 512), F32)
        SMALL = AR.alloc("SMALL", (128, 64), F32)
        S_.copy("dve", IDB, con("ident")); S_.memset("dve", ONESD, 1.0 / 1024); S_.memset("dve", ONESC, 1.0 / 512)
        S_.copy("dve", MPREV, con("mprev")); S_.copy("dve", MNEXT, con("mnext"))
        S_.memset("pool", VA, 1.0); S_.memset("pool", QA, 0.0); S_.memset("pool", QB, 0.0)

        def wload(src_view, shape3):
            slot = WR[wst["i"] % NWR]; wst["i"] += 1
            a, b = shape3
            v = slot[:, 0:a * b]
            v = v.re(v.ap.rearrange("p (a b) -> p a b", b=b))
            S_.dma("sp", v, src_view)
            return v

        def ws_gran(l, g):
            ap = WS_d[l].ap().rearrange("(kt p) n -> p kt n", p=128)[:, :, g * 512:(g + 1) * 512]
            return dview(WS_d[l], ap, ((0, D), (g * 512, (g + 1) * 512)))

        with AR.scope():
            Z = AR.alloc("Z", (128, 4, 32), BF16)
            S_.memset("dve", Z, 0.0)
            for U_d in (USC_d, UCF_d):
                for (lo, w) in ((0, 16), (16 + L, 32), (NTP - 16, 16)):
                    ap = U_d.ap().rearrange("(c p) t -> p c t", p=128)[:, :, lo:lo + w]
                    S_.dma("sp", dview(U_d, ap, ((0, BW), (lo, lo + w))), Z[:, :, 0:w])
        def xblock_src(l, g0, n, isc):
            if l == 0:
                t, o = (cT_d, g0) if isc else (xT_d, g0 - L)
            else:
                t, o = XS_d[l % 2], g0
            return dview(t, t.ap().rearrange("(j p) t -> p j t", p=128)[:, :, o:o + n], ((0, D), (o, o + n)))

        def xchunk_src(l, oc, g0, n, isc):
            if l == 0:
                t, o = (cT_d, g0) if isc else (xT_d, g0 - L)
            else:
                t, o = XS_d[l % 2], g0
            return dview(t, t.ap()[oc * 128:(oc + 1) * 128, o:o + n], ((oc * 128, oc * 128 + 128), (o, o + n)))

        def cast(dst_t, dst_ap, dbox, src_t, src_ap):
            S_.dma("pool", dview(dst_t, dst_ap, dbox), dview(src_t, src_ap, ((0, 1),)))

        def cast_layer(l):
            W = win_d.ap()[l]; Wd = WS_d[l].ap()
            def seg(gc, c0, n, src0):
                d0 = gc
                cast(WS_d[l], Wd[:, d0:d0 + n], ((0, D), (d0, d0 + n)), win_d, W[:, src0:src0 + n])
            names = ['ret_q','ret_k','ret_v','ret_z','sc_b','sc_c','sc_x','sc_z','cf_a','cf_g','cf_z']
            for gi in (1, 2, 5, 6, 8, 9):
                seg(gi * 512, 0, 512, gi * 512)
            aq = IN_OFF['att_q']
            for m in range(4):
                for hh, h in enumerate((m, 4 + m)):
                    seg(G_AQ * 512 + m * 128 + hh * 64, 0, 64, aq + h * 64)
                    seg(G_AQS * 512 + m * 128 + hh * 64, 0, 32, aq + h * 64 + 32)
                    seg(G_AQS * 512 + m * 128 + hh * 64 + 32, 0, 32, aq + h * 64)
            ak = IN_OFF['att_k']; av = IN_OFF['att_v']
            seg(G_AKV * 512, 0, 128, ak)
            for g in range(2):
                seg(G_AKV * 512 + 128 + g * 64, 0, 32, ak + g * 64 + 32)
                seg(G_AKV * 512 + 128 + g * 64 + 32, 0, 32, ak + g * 64)
            seg(G_AKV * 512 + 256, 0, 128, av)
            seg(G_AKV * 512 + 384, 0, 128, av)
            for gi in (0, 3, 4, 7, 10):
                seg(gi * 512, 0, 512, gi * 512)
            seg(G_AZ * 512, 0, 512, IN_OFF['att_z'])
            seg(G_GATE * 512, 0, 4096, IN_OFF['gates'])
            cast(WB_d[l], WB_d[l].ap(), ((0, 4), (0, BW), (0, D)), wbr_d, wbr_d.ap()[l])
            cast(WO_d[l], WO_d[l].ap(), ((0, D), (0, D)), wout_d, wout_d.ap()[l])

        stage(0)

        SC = AR.alloc("SC", (128, 8, 2), BF16)
        S_.act(SC, par("cvec"), AF.Silu)

        def do_mod(l):
            for jc in range(6):
                slot = WR[wst["i"] % NWR]; wst["i"] += 1
                wv = slot.re(slot.ap.rearrange("p (a b) -> p a b", b=512))
                src = wmod_d.ap()[l].rearrange("(kt p) n -> p kt n", p=128)[:, :, jc * 512:(jc + 1) * 512]
                S_.dma("pool", wv, dview(wmod_d, src, ((0, 1),)))
                P = ps()
                for j4 in range(4):
                    for kt in range(8):
                        S_.mm(P[:, j4 * 2:j4 * 2 + 2], wv[:, kt, j4 * 128:(j4 + 1) * 128], SC[:, kt, :], start=(kt == 0), stop=(kt == 7))
                for j4 in range(4):
                    j = jc * 4 + j4
                    S_.ts("dve", MOD[:, l, j, :], P[:, j4 * 2:j4 * 2 + 2], par("b_mod", l, slice(j, j + 1)), None, ALU.add)
            for w in range(2):
                S_.stt("dve", AXC[:, l, w, :], MOD[:, l, 8:16, w], 1.0, par("norm_w", l), ALU.add, ALU.mult)

        def do_diag(l):
            saved = AR.top
            AR.top = NW - 1984
            m = AR.mark()
            stg_ = [AR.alloc("stg0", (128, 31, 128), BF16)]
            for gi in range(5):
                stg = stg_[0]
                nt = 31 if gi < 4 else 12
                for k in range(nt):
                    wcol = par("cfw", l, gi, slice(k, k + 1)) if gi < 4 else par("scw", l, k // 3, slice(k % 3, k % 3 + 1))
                    S_.ts("pool", stg[:, k, :], con("ident"), wcol, None, ALU.mult)
                ap = CD_d[l].ap()[gi].rearrange("p (k q) -> p k q", q=128)[:, 0:nt, :]
                S_.dma("pool", dview(CD_d[l], ap, ((gi, gi + 1),)), stg[:, 0:nt, :])
            AR.release(m)
            AR.top = saved

        do_mod(0)
        stage(1)
        for l in range(DEPTH):
            cast_layer(l)
            do_diag(l)
        stage(3)

        def nscope(name):
            return nc.named_scope(name) if DEBUG else contextlib.nullcontext()

        def Avec(l, w, j): return AXC[:, l, w, j:j + 1]
        def Bvec(l, w, j): return MOD[:, l, j, w:w + 1]
        def Gvec(l, w, j): return MOD[:, l, 16 + j, w:w + 1]

        def proj(wv, c0, blk, n, out=None):
            P = out if out is not None else ps()
            for kt in range(8):
                S_.mm(P[:, 0:n], wv[:, kt, c0:c0 + 128], HX[:, kt, blk], start=(kt == 0), stop=(kt == 7))
            return P

        for l in range(DEPTH):
            last = (l == DEPTH - 1)
            cur, nxt = XS_d[l % 2], XS_d[(l + 1) % 2]
            with AR.scope():
                t0 = AR.alloc("t0", (128, 8), F32); e1 = AR.alloc("e1", (128, 128), F32); e2 = AR.alloc("e2", (128, 128), F32)
                kw1 = AR.alloc("kw1", (128, 8), F32)
                S_.act(t0, par("rdec", l), AF.Exp, scale=-1.0)
                S_.act(t0, t0, AF.Ln, bias=1.0)
                S_.ts("dve", LG, t0, -1.0, None, ALU.mult)
                S_.act(ESINK, par("sink", l), AF.Exp)
                for h in range(4):
                    S_.act(e1, con("posdF"), AF.Exp, scale=LG[:, h:h + 1])
                    S_.tt("dve", e1, e1, con("MF"), ALU.mult)
                    S_.act(e2, con("posdB"), AF.Exp, scale=LG[:, 4 + h:5 + h])
                    S_.tt("dve", e2, e2, con("MB"), ALU.mult)
                    S_.tt("dve", e1, e1, e2, ALU.add)
                    S_.ts("dve", DT[:, h, :], e1, KS_RET, None, ALU.mult)
                    S_.act(QWF[:, h, :], con("rampF1"), AF.Exp, scale=LG[:, h:h + 1])
                    S_.act(QWB[:, h, :], con("rampB"), AF.Exp, scale=LG[:, 4 + h:5 + h])
                    S_.act(kw1[:, h:h + 1], con("rp127", 1), AF.Exp, scale=LG[:, h:h + 1])
                    S_.act(kw1[:, 4 + h:5 + h], con("rp", 1), AF.Exp, scale=LG[:, 4 + h:5 + h])
                    S_.ts("dve", KWF[:, h * 128:(h + 1) * 128], con("onesf"), kw1[:, h:h + 1], KS_RET, ALU.mult, ALU.mult)
                    S_.ts("dve", KWB[:, h * 128:(h + 1) * 128], con("onesf"), kw1[:, 4 + h:5 + h], KS_RET, ALU.mult, ALU.mult)
                    S_.act(GF[:, h * 128:(h + 1) * 128], con("c128"), AF.Exp, scale=LG[:, h:h + 1])
                    S_.act(GB[:, h * 128:(h + 1) * 128], con("c128"), AF.Exp, scale=LG[:, 4 + h:5 + h])
            stage(10 + 100 * l)
            with AR.scope():
                sets = []
                sqw = AR.alloc("sqw", (128, 8, TB), BF16)
                for k in range(2):
                    sets.append(dict(xb=AR.alloc(f"xb{k}", (128, 8, TB), F32), rstd=AR.alloc(f"rstd{k}", (128, TB), F32),
                                     tmp=[AR.alloc(f"tmp{k}{q}", (128, TB), F32) for q in range(1)]))
                for bi_, (g0, n, isc) in enumerate(blocks):
                    w = 1 if isc else 0
                    B_ = sets[bi_ % 2]
                    with nscope(f"p1_L{l}_{g0}"):
                        xb = B_["xb"][:, :, 0:n]; rstd = B_["rstd"][:, 0:n]; sq = sqw[:, :, 0:n]
                        S_.dma("sp", xb, xblock_src(l, g0, n, isc))
                        S_.act(sq, xb, AF.Square)
                        P = ps()
                        for j in range(8):
                            S_.mm(P[:, 0:n], ONESD, sq[:, j, :], start=(j == 0), stop=(j == 7))
                        S_.act(rstd, P[:, 0:n], AF.Sqrt, bias=EPS)
                        S_.recip(rstd, rstd)
                        for j in range(8):
                            tmp = B_["tmp"][0][:, 0:n]
                            S_.stt("dve", tmp, xb[:, j, :], Avec(l, w, j), rstd, ALU.mult, ALU.mult)
                            S_.ts("dve", HX[:, j, g0:g0 + n], tmp, Bvec(l, w, j), None, ALU.add)

            stage(11 + 100 * l)
            fwd = list(range(NCH))
            bwd = list(range(NCC - 1, -1, -1)) + list(range(NCH - 1, NCC - 1, -1))
            for (order, KW_, G_, ST_d) in ((fwd, KWF, GF, SF_d), (bwd, KWB, GB, SB_d)):
                with AR.scope(), nscope(f"p2_L{l}_{ST_d.name}"):
                    Wk = wload(ws_gran(l, G_RK), (8, 512)); Wv = wload(ws_gran(l, G_RV), (8, 512))
                    ST = AR.alloc("ST", (128, 512), F32); S_.memset("dve", ST, 0.0)
                    tm = AR.alloc("tm", (128, 512), F32)
                    kwb_ = [AR.alloc(f"kw{i}", (128, 512), BF16) for i in range(2)]
                    vb_ = [AR.alloc(f"vb{i}", (128, 512), BF16) for i in range(2)]
                    sb_ = [AR.alloc(f"sb{i}", (128, 512), BF16) for i in range(2)]
                    for ci, c in enumerate(order):
                        kp = ps(); vp = ps()
                        for kt in range(8):
                            S_.mm(kp, HX[:, kt, c * 128:(c + 1) * 128], Wk[:, kt, :], start=(kt == 0), stop=(kt == 7))
                        for kt in range(8):
                            S_.mm(vp, HX[:, kt, c * 128:(c + 1) * 128], Wv[:, kt, :], start=(kt == 0), stop=(kt == 7))
                        kw = kwb_[ci % 2]; vb = vb_[ci % 2]; sb = sb_[ci % 2]
                        S_.tt("dve", kw, kp, KW_, ALU.mult)
                        S_.copy("act", vb, vp)
                        kv = ps()
                        for h in range(4):
                            S_.mm(kv[:, h * 128:(h + 1) * 128], kw[:, h * 128:(h + 1) * 128], vb[:, h * 128:(h + 1) * 128])
                        S_.copy("act", sb, ST)
                        S_.dma("sp", dview(ST_d, ST_d.ap()[c], ((c, c + 1),)), sb)
                        S_.tt("dve", tm, ST, G_, ALU.mult)
                        S_.tt("dve", ST, tm, kv, ALU.add)

            stage(12 + 100 * l)
            with AR.scope(), nscope(f"p2b_L{l}"):
                Wc = wload(ws_gran(l, G_SC), (8, 512)); Wx = wload(ws_gran(l, G_SX), (8, 512))
                Wa = wload(ws_gran(l, G_CA), (8, 512)); Wg = wload(ws_gran(l, G_CG), (8, 512))
                for (g0, n, isc) in blocks:
                    blk = slice(g0, g0 + n)
                    with AR.scope():
                        usc = AR.alloc("usc", (128, 4, n), BF16); ucf = AR.alloc("ucf", (128, 4, n), BF16)
                        xs_ = [AR.alloc(f"xs{i}", (128, n), F32) for i in range(2)]
                        for ch in range(4):
                            pc = proj(Wc, ch * 128, blk, n); px = proj(Wx, ch * 128, blk, n)
                            t = xs_[0]
                            S_.copy("act", t, px[:, 0:n])
                            S_.tt("dve", usc[:, ch, :], pc[:, 0:n], t, ALU.mult)
                            pa = proj(Wa, ch * 128, blk, n); pg = proj(Wg, ch * 128, blk, n)
                            t = xs_[1]
                            S_.act(t, pg[:, 0:n], AF.Sigmoid)
                            S_.tt("dve", ucf[:, ch, :], pa[:, 0:n], t, ALU.mult)
                        p0 = pidx(g0)
                        for U_d, u in ((USC_d, usc), (UCF_d, ucf)):
                            ap = U_d.ap().rearrange("(c p) t -> p c t", p=128)[:, :, p0:p0 + n]
                            S_.dma("sp", dview(U_d, ap, ((0, BW), (p0, p0 + n))), u)

            stage(13 + 100 * l)
            with AR.scope(), nscope(f"p2c_L{l}"):
                Wkv = wload(ws_gran(l, G_AKV), (8, 512))
                for (g0, n, isc) in blocks:
                    blk = slice(g0, g0 + n)
                    with AR.scope():
                        cs = AR.alloc("cs", (128, 2, n), F32); t1 = AR.alloc("t1", (128, n), F32); t2 = AR.alloc("t2", (128, n), F32)
                        src = rope_d.ap().rearrange("a p t -> p a t")[:, :, g0:g0 + n]
                        S_.dma("sp", cs, dview(rope_d, src, ((0, 1),)))
                        pk = proj(Wkv, 0, blk, n); pks = proj(Wkv, 128, blk, n)
                        S_.tt("dve", t1, pk[:, 0:n], cs[:, 0, :], ALU.mult)
                        S_.tt("dve", t2, pks[:, 0:n], cs[:, 1, :], ALU.mult)
                        S_.tt("dve", KR[:, blk], t1, t2, ALU.add)
                        for c in range(g0 // 128, (g0 + n) // 128):
                            pvv = ps()
                            for kt in range(8):
                                S_.mm(pvv[:, 0:128], HX[:, kt, c * 128:(c + 1) * 128], Wkv[:, kt, 256:384], start=(kt == 0), stop=(kt == 7))
                            src3 = pvv[:, 0:128]; src3 = src3.re(src3.ap.rearrange("p (g e) -> p g e", e=64))
                            S_.copy("act", VA[:, c, :, 0:64], src3)

            if l + 1 < DEPTH:
                do_mod(l + 1)
            stage(14 + 100 * l)
            for (g0, n, isc) in blocks:
                if last and isc:
                    continue
                w = 1 if isc else 0
                blk = slice(g0, g0 + n); nchk = n // 128; c0 = g0 // 128
                with AR.scope():
                    BR = AR.alloc("BR", (128, 4, n), BF16)
                    accbox = {}

                    def merge(i):
                        if DEBUG:
                            if "DBG" not in dmem: dmem["DBG"] = Mem("DBG")
                            S_.dma("sp", dview(DBG_d, DBG_d.ap()[i].rearrange("(c p) t -> p c t", p=128)[:, :, g0:g0 + n], ((i, i + 1), (0, BW), (g0, g0 + n))), BR)
                        Wb = wload(dview(WB_d[l], WB_d[l].ap()[i].rearrange("(kt p) n -> p kt n", p=128), ((i, i + 1),)), (4, 1024))
                        with AR.scope(), nscope(f"mrg{i}_L{l}_{g0}"):
                            sg_ = [AR.alloc(f"sg{k}", (128, n), F32) for k in range(2)]
                            tt_ = [AR.alloc(f"tt{k}", (128, n), F32) for k in range(2)]
                            for half in range(2):
                                Wg_ = wload(ws_gran(l, G_GATE + 2 * i + half), (8, 512))
                                for o4 in range(4):
                                    oc = half * 4 + o4
                                    Gp = proj(Wg_, o4 * 128, blk, n)
                                    Pp = ps()
                                    for kt in range(4):
                                        S_.mm(Pp[:, 0:n], Wb[:, kt, oc * 128:(oc + 1) * 128], BR[:, kt, :], start=(kt == 0), stop=(kt == 3))
                                    sg = sg_[oc % 2]; tq = tt_[oc % 2]
                                    S_.act(sg, Gp[:, 0:n], AF.Sigmoid)
                                    if i == 0:
                                        S_.tt("dve", accbox['a'][:, oc, :], Pp[:, 0:n], sg, ALU.mult)
                                    else:
                                        S_.tt("dve", tq, Pp[:, 0:n], sg, ALU.mult)
                                        dst = accbox['m'][:, oc, :] if i == 3 else accbox['a'][:, oc, :]
                                        S_.tt("dve", dst, accbox['a'][:, oc, :], tq, ALU.add)


                    def tok2feat(on, zs, col0):
                        T = ps(BF16, (128, 4, 128))
                        for k in range(4):
                            S_.transpose(T[:, k, :], on[:, k * 128:(k + 1) * 128], IDB, signal=(k == 3))
                        S_.tt("dve", BR[:, :, col0:col0 + 128], T, zs[:, :, col0:col0 + 128], ALU.mult)

                    with AR.scope(), nscope(f"b3a_L{l}_{g0}"):
                        Wq = wload(ws_gran(l, G_RQ), (8, 512)); Wk = wload(ws_gran(l, G_RK), (8, 512))
                        Wz = wload(ws_gran(l, G_RZ), (8, 512)); Wv = wload(ws_gran(l, G_RV), (8, 512))
                        qT = AR.alloc("qT", (128, 4, n), BF16); kT = AR.alloc("kT", (128, 4, n), BF16)
                        zs = AR.alloc("zs", (128, 4, n), BF16); vt = AR.alloc("vt", (128, nchk, 512), BF16)
                        SFb = AR.alloc("SFb", (128, nchk, 512), BF16); SBb = AR.alloc("SBb", (128, nchk, 512), BF16)
                        for ST_d, dst in ((SF_d, SFb), (SB_d, SBb)):
                            S_.dma("sp", dst, dview(ST_d, ST_d.ap()[c0:c0 + nchk].rearrange("c p f -> p c f"), ((c0, c0 + nchk),)))
                        for h in range(4):
                            P = proj(Wq, h * 128, blk, n); S_.copy("act", qT[:, h, :], P[:, 0:n])
                        for h in range(4):
                            P = proj(Wk, h * 128, blk, n); S_.copy("act", kT[:, h, :], P[:, 0:n])
                        for h in range(4):
                            P = proj(Wz, h * 128, blk, n); S_.act(zs[:, h, :], P[:, 0:n], AF.Silu)
                            S_.ts("dve", zs[:, h, :], zs[:, h, :], par("gn_w", l, slice(h, h + 1)), None, ALU.mult)
                        for ci in range(nchk):
                            P = ps()
                            for kt in range(8):
                                S_.mm(P, HX[:, kt, g0 + ci * 128:g0 + (ci + 1) * 128], Wv[:, kt, :], start=(kt == 0), stop=(kt == 7))
                            S_.copy("act", vt[:, ci, :], P)
                        stage(15 + 100 * l)
                        sTm_ = [AR.alloc(f"sTm{k}", (128, 4, 128), BF16) for k in range(2)]
                        qf_ = [AR.alloc(f"qf{k}", (128, 4, 128), BF16) for k in range(2)]
                        qb_ = [AR.alloc(f"qb{k}", (128, 4, 128), BF16) for k in range(2)]
                        on_ = [AR.alloc(f"on{k}", (128, 512), BF16) for k in range(2)]
                        sqt = AR.alloc("sqt", (128, 4, 128), F32)
                        st_ = [AR.alloc(f"st{k}", (128, 16), F32) for k in range(2)]
                        def stA(ci):
                            cs_ = slice(ci * 128, (ci + 1) * 128)
                            sTm = sTm_[ci % 2]; qf = qf_[ci % 2]; qb = qb_[ci % 2]
                            Ps = ps(F32, (128, 4, 128))
                            for h in range(4):
                                S_.mm(Ps[:, h, :], kT[:, h, cs_], qT[:, h, cs_])
                            S_.tt("dve", sTm, Ps, DT, ALU.mult)
                            S_.tt("dve", qf, qT[:, :, cs_], QWF, ALU.mult)
                            S_.tt("dve", qb, qT[:, :, cs_], QWB, ALU.mult)

                        def stB(ci):
                            sTm = sTm_[ci % 2]; qf = qf_[ci % 2]; qb = qb_[ci % 2]; on = on_[ci % 2]; st = st_[ci % 2]
                            Po = ps(F32, (128, 4, 128))
                            for h in range(4):
                                hs = slice(h * 128, (h + 1) * 128)
                                S_.mm(Po[:, h, :], sTm[:, h, :], vt[:, ci, hs], start=True, stop=False)
                                S_.mm(Po[:, h, :], qf[:, h, :], SFb[:, ci, hs], start=False, stop=False)
                                S_.mm(Po[:, h, :], qb[:, h, :], SBb[:, ci, hs], start=False, stop=True)
                            S_.reduce("dve", st[:, 0:4], Po, ALU.add)
                            S_.act(sqt, Po, AF.Square)
                            S_.reduce("dve", st[:, 4:8], sqt, ALU.add)
                            S_.ts("dve", st[:, 0:4], st[:, 0:4], 1.0 / 128, None, ALU.mult)
                            S_.tt("dve", st[:, 8:12], st[:, 0:4], st[:, 0:4], ALU.mult)
                            S_.stt("dve", st[:, 4:8], st[:, 4:8], 1.0 / 128, st[:, 8:12], ALU.mult, ALU.subtract)
                            S_.act(st[:, 4:8], st[:, 4:8], AF.Sqrt, bias=EPS)
                            S_.recip(st[:, 4:8], st[:, 4:8])
                            for h in range(4):
                                S_.ts("dve", on[:, h * 128:(h + 1) * 128], Po[:, h, :], st[:, h:h + 1], st[:, 4 + h:5 + h], ALU.subtract, ALU.mult)

                        def stC(ci):
                            tok2feat(on_[ci % 2], zs, ci * 128)

                        stA(0)
                        for ci in range(nchk):
                            if ci + 1 < nchk:
                                stA(ci + 1)
                            stB(ci)
                            if ci >= 1:
                                stC(ci - 1)
                        stC(nchk - 1)
                    stage(20 + 100 * l)
                    ACC = AR.alloc("ACC", (128, 8, n), F32); accbox['a'] = ACC
                    MB16 = AR.alloc("MB16", (128, 8, n), BF16); accbox['m'] = MB16
                    merge(0)

                    stage(21 + 100 * l)
                    with AR.scope(), nscope(f"b3b_L{l}_{g0}"):
                        Wz = wload(ws_gran(l, G_SZ), (8, 512)); Wb_ = wload(ws_gran(l, G_SB), (8, 512))
                        cd = wload(dview(CD_d[l], CD_d[l].ap()[4].rearrange("p (k q) -> p k q", q=128), ((4, 5),)), (31, 128))
                        U = AR.alloc("U", (128, 4, n + 2), BF16)
                        p0 = pidx(g0) - 1
                        S_.dma("sp", U, dview(USC_d, USC_d.ap().rearrange("(c p) t -> p c t", p=128)[:, :, p0:p0 + n + 2], ((0, BW), (p0, p0 + n + 2))))
                        bs = AR.alloc("bs", (128, 4, n), F32); zz = AR.alloc("zz", (128, 4, n), BF16)
                        for ch in range(4):
                            Pz = proj(Wz, ch * 128, blk, n)
                            S_.act(zz[:, ch, :], Pz[:, 0:n], AF.Silu)
                        for ch in range(4):
                            Pb = proj(Wb_, ch * 128, blk, n)
                            S_.tt("dve", bs[:, ch, :], Pb[:, 0:n], zz[:, ch, :], ALU.mult)
                        for ch in range(4):
                            Pc = ps()
                            for k in range(3):
                                S_.mm(Pc[:, 0:n], cd[:, ch * 3 + k, :], U[:, ch, k:k + n], start=(k == 0), stop=(k == 2))
                            S_.tt("dve", BR[:, ch, :], Pc[:, 0:n], bs[:, ch, :], ALU.mult)
                    merge(1)

                    stage(22 + 100 * l)
                    with AR.scope(), nscope(f"b3c_L{l}_{g0}"):
                        zz = AR.alloc("zz", (128, 4, n), BF16)
                        uf = AR.alloc("uf", (128, 4, n), F32)
                        mean = AR.alloc("mean", (128, n), F32); rs = AR.alloc("rs", (128, n), F32)
                        with AR.scope():
                            U = AR.alloc("U", (128, 4, n + 30), BF16)
                            ub_ = [AR.alloc(f"ub{k}", (128, n), BF16) for k in range(2)]
                            us_ = [AR.alloc(f"us{k}", (128, n), BF16) for k in range(2)]
                            p0 = pidx(g0) - 15
                            S_.dma("sp", U, dview(UCF_d, UCF_d.ap().rearrange("(c p) t -> p c t", p=128)[:, :, p0:p0 + n + 30], ((0, BW), (p0, p0 + n + 30))))
                            Pm = pv(6); Pq = pv(7)
                            for ch in range(4):
                                cd = wload(dview(CD_d[l], CD_d[l].ap()[ch].rearrange("p (k q) -> p k q", q=128), ((ch, ch + 1),)), (31, 128))
                                Pc = ps()
                                for k in range(31):
                                    S_.mm(Pc[:, 0:n], cd[:, k, :], U[:, ch, k:k + n], start=(k == 0), stop=(k == 30))
                                bcol = par("cfb", l, slice(ch, ch + 1))
                                ub = ub_[ch % 2]; us = us_[ch % 2]
                                S_.act(uf[:, ch, :], Pc[:, 0:n], AF.Identity, bias=bcol)
                                S_.act(us, Pc[:, 0:n], AF.Square, bias=bcol)
                                S_.copy("dve", ub, uf[:, ch, :])
                                S_.mm(Pm[:, 0:n], ONESC, ub, start=(ch == 0), stop=(ch == 3))
                                S_.mm(Pq[:, 0:n], ONESC, us, start=(ch == 0), stop=(ch == 3))
                            Wz = wload(ws_gran(l, G_CZ), (8, 512))
                            for ch in range(4):
                                Pz = proj(Wz, ch * 128, blk, n)
                                S_.act(zz[:, ch, :], Pz[:, 0:n], AF.Silu)
                            S_.copy("act", mean, Pm[:, 0:n])
                            S_.act(rs, Pm[:, 0:n], AF.Square)
                            S_.tt("dve", rs, Pq[:, 0:n], rs, ALU.subtract)
                            S_.act(rs, rs, AF.Sqrt, bias=EPS)
                            S_.recip(rs, rs)
                        t1_ = [AR.alloc(f"t1{k}", (128, n), F32) for k in range(2)]
                        for ch in range(4):
                            t1 = t1_[ch % 2]
                            S_.tt("dve", t1, uf[:, ch, :], mean, ALU.subtract)
                            S_.tt("dve", t1, t1, rs, ALU.mult)
                            S_.act(t1, t1, AF.Silu, bias=par("lnb", l, slice(ch, ch + 1)), scale=par("lnw", l, slice(ch, ch + 1)))
                            S_.tt("dve", BR[:, ch, :], t1, zz[:, ch, :], ALU.mult)
                    merge(2)

                    stage(23 + 100 * l)
                    with AR.scope(), nscope(f"b3d_L{l}_{g0}"):
                        Wq = wload(ws_gran(l, G_AQ), (8, 512)); Wqs = wload(ws_gran(l, G_AQS), (8, 512)); Wz = wload(ws_gran(l, G_AZ), (8, 512))
                        cs = AR.alloc("cs", (128, 2, n), F32); t1 = AR.alloc("t1", (128, 4, n), F32); t2 = AR.alloc("t2", (128, n), F32)
                        zs = AR.alloc("zs", (128, 4, n), BF16)
                        S_.dma("sp", cs, dview(rope_d, rope_d.ap().rearrange("a p t -> p a t")[:, :, g0:g0 + n], ((0, 1),)))
                        for m in range(4):
                            Pq_ = proj(Wq, m * 128, blk, n)
                            S_.tt("dve", t1[:, m, :], Pq_[:, 0:n], cs[:, 0, :], ALU.mult)
                        for m in range(4):
                            Pqs = proj(Wqs, m * 128, blk, n)
                            S_.tt("dve", t2, Pqs[:, 0:n], cs[:, 1, :], ALU.mult)
                            S_.tt("dve", QA[0:64, m, 0:n], t1[0:64, m, :], t2[0:64, :], ALU.add)
                            S_.tt("dve", QB[64:128, m, 0:n], t1[64:128, m, :], t2[64:128, :], ALU.add)
                        for m in range(4):
                            Pz = proj(Wz, m * 128, blk, n)
                            S_.act(zs[:, m, :], Pz[:, 0:n], AF.Silu)
                        pt_ = [AR.alloc(f"pt{k}", (128, 4, 128), BF16) for k in range(3)]
                        on_ = [AR.alloc(f"on{k}", (128, 8, 64), BF16) for k in range(2)]
                        dn_ = [AR.alloc(f"dn{k}", (128, 8), F32) for k in range(2)]
                        pti = 0
                        for qi in range(nchk):
                            qc = c0 + qi; qs_ = slice(qi * 128, (qi + 1) * 128)
                            if isc:
                                kbs = [(c, None) for c in range(NCC)]
                            else:
                                kbs = []
                                if qc - 1 >= NCC: kbs.append((qc - 1, MPREV))
                                kbs.append((qc, None))
                                if qc + 1 < NCH: kbs.append((qc + 1, MNEXT))
                                kbs += [(c, None) for c in range(NCC)]
                            on = on_[qi % 2]; dn = dn_[qi % 2]
                            Og = [pv(6, F32, (128, 4, 65)), pv(7, F32, (128, 4, 65))]
                            steps = [(bi, kb, msk, g) for bi, (kb, msk) in enumerate(kbs) for g in range(2)]
                            QQ = (QA, QB)

                            def score(step):
                                bi, kb, msk, g = step
                                Pst = ps(F32, (128, 4, 128))
                                S_.mm(Pst, KR[:, kb * 128:(kb + 1) * 128], QQ[g][:, :, qs_])
                                return Pst
                            pend = score(steps[0])
                            for si, step in enumerate(steps):
                                bi, kb, msk, g = step
                                Pst = pend
                                if si + 1 < len(steps):
                                    pend = score(steps[si + 1])
                                pt = pt_[pti % 3]; pti += 1
                                S_.act(pt, Pst, AF.Exp, scale=ATT_SCALE)
                                if msk is not None:
                                    mb = msk.re(msk.ap.unsqueeze(1).to_broadcast([128, 4, 128]))
                                    S_.tt("dve", pt, pt, mb, ALU.mult)
                                for hl in range(4):
                                    S_.mm(Og[g][:, hl, :], pt[:, hl, :], VA[:, kb, g, :], start=(bi == 0 and hl == 0), stop=(bi == len(kbs) - 1),
                                          signal=(bi == len(kbs) - 1 and hl == 3))
                            for g in range(2):
                                O = Og[g]
                                es_ = ESINK[:, g * 4:(g + 1) * 4]
                                S_.tt("dve", dn[:, g * 4:(g + 1) * 4], O[:, :, 64], es_, ALU.add)
                                S_.recip(dn[:, g * 4:(g + 1) * 4], dn[:, g * 4:(g + 1) * 4])
                                dsl = dn[:, g * 4:(g + 1) * 4]
                                rb = dsl.re(dsl.ap.unsqueeze(2).to_broadcast([128, 4, 64]))
                                S_.tt("dve", on[:, g * 4:(g + 1) * 4, :], O[:, :, 0:64], rb, ALU.mult)
                            onf = on.re(on.ap.rearrange("p h e -> p (h e)"))
                            T = ps(BF16, (128, 4, 128))
                            for k in range(4):
                                S_.transpose(T[:, k, :], onf[:, k * 128:(k + 1) * 128], IDB, signal=(k == 3))
                            S_.tt("dve", BR[:, :, qs_], T, zs[:, :, qs_], ALU.mult)
                    merge(3)

                    stage(24 + 100 * l)
                    with AR.scope(), nscope(f"b3e_L{l}_{g0}"):
                        XN = ACC
                        if last:
                            sq = AR.alloc("sq", (128, 8, n), BF16)
                        Wo_ = [wload(dview(WO_d[l], WO_d[l].ap().rearrange("(kt p) n -> p kt n", p=128)[:, :, half * 512:(half + 1) * 512], ((0, D), (half * 512, half * 512 + 512))), (8, 512)) for half in range(2)]
                        for oc in range(8):
                            S_.dma("sp", XN[:, oc, :], xchunk_src(l, oc, g0, n, isc))
                        for half in range(2):
                            Wo = Wo_[half]
                            for o4 in range(4):
                                oc = half * 4 + o4
                                Y = ps()
                                for kt in range(8):
                                    S_.mm(Y[:, 0:n], Wo[:, kt, o4 * 128:(o4 + 1) * 128], MB16[:, kt, :], start=(kt == 0), stop=(kt == 7))
                                S_.stt("dve", XN[:, oc, :], Y[:, 0:n], Gvec(l, w, oc), XN[:, oc, :], ALU.mult, ALU.add)
                                if not last:
                                    S_.dma(STQ, dview(nxt, nxt.ap()[oc * 128:(oc + 1) * 128, g0:g0 + n], ((oc * 128, oc * 128 + 128), (g0, g0 + n))), XN[:, oc, :])
                        if last:
                            S_.act(sq, XN, AF.Square)
                            P = ps()
                            for j in range(8):
                                S_.mm(P[:, 0:n], ONESD, sq[:, j, :], start=(j == 0), stop=(j == 7))
                            rstd = AR.alloc("rstd", (128, n), F32)
                            S_.act(rstd, P[:, 0:n], AF.Sqrt, bias=EPS)
                            S_.recip(rstd, rstd)
                            for j in range(8):
                                S_.stt("dve", XN[:, j, :], XN[:, j, :], par("fnw", slice(j, j + 1)), rstd, ALU.mult, ALU.mult)
                            t0_ = g0 - L
                            S_.dma("sp", dview(yT_d, yT_d.ap().rearrange("(j p) t -> p j t", p=128)[:, :, t0_:t0_ + n], ((0, D), (t0_, t0_ + n))), XN)
        S_.wait_all("sp", [dmem["yT"]])
        print(f"[build] ops={S_.nops} waits={S_.nwaits} sems={S_.nsem} arena_top={AR.top}")
    return nc


def make_in_maps(inp, S, L, DEPTH, nb):
    C, R = host_consts(S, L)
    maps = []
    for b in range(nb):
        maps.append({
            "xT": np.ascontiguousarray(inp["x"][b].T), "cT": np.ascontiguousarray(inp["ctx"][b].T),
            "par": host_params(inp, b, DEPTH), "con": C, "rope": R,
            "w_mod": inp["w_mod"][:DEPTH], "w_in": inp["w_in"][:DEPTH], "w_branch": inp["w_branch"][:DEPTH], "w_out": inp["w_out"][:DEPTH],
        })
    return maps


def kernel(**inputs):
    from concourse.bass_utils import run_bass_kernel_spmd
    inp = {k: np.asarray(v) for k, v in inputs.items()}
    B, S, _ = inp["x"].shape; L = inp["ctx"].shape[1]; DEPTH = inp["w_in"].shape[0]
    nc = build_program(S, L, DEPTH)
    maps = make_in_maps(inp, S, L, DEPTH, B)
    res = run_bass_kernel_spmd(nc, maps, core_ids=list(range(B)))
    out = np.stack([np.ascontiguousarray(r["yT"].T) for r in res.results], 0)
    return out.astype(np.float32)
```

```python
import numpy as np, contextlib
import concourse.bass as bass
import concourse.mybir as mybir

F32 = mybir.dt.float32
BF16 = mybir.dt.bfloat16
AF = mybir.ActivationFunctionType
ALU = mybir.AluOpType
AX = mybir.AxisListType
_ISZ = {F32: 4, BF16: 2}


def _norm_idx(idx, shape):
    if not isinstance(idx, tuple):
        idx = (idx,)
    idx = list(idx) + [slice(None)] * (len(shape) - len(idx))
    out = []
    for i, n in zip(idx, shape):
        if isinstance(i, int):
            out.append((i, i + 1, True))
        else:
            lo = 0 if i.start is None else i.start
            hi = n if i.stop is None else i.stop
            assert i.step in (None, 1) and 0 <= lo < hi <= n, (i, n)
            out.append((lo, hi, False))
    return out


class Mem:
    def __init__(self, name, birth=None, excl=False):
        self.name = name
        self.excl = excl
        self.recs = []
        self.birth = dict(birth or {})

    def summary(self):
        d = dict(self.birth)
        for _, w, r in self.recs:
            _merge(d, w)
            _merge(d, r)
        return d


def _merge(d, s):
    for k, v in s.items():
        if d.get(k, 0) < v:
            d[k] = v


def _ovl(a, b):
    for (al, ah), (bl, bh) in zip(a, b):
        if al >= bh or bl >= ah:
            return False
    return True


def _covers(a, b):
    for (al, ah), (bl, bh) in zip(a, b):
        if al > bl or ah < bh:
            return False
    return True


class V:
    __slots__ = ("ap", "mem", "box", "shape", "base")

    def __init__(self, ap, mem, box, shape, base=None):
        self.ap = ap
        self.mem = mem
        self.box = box
        self.shape = shape
        self.base = base

    def __getitem__(self, idx):
        if self.base is None:
            return V(self.ap[idx], self.mem, self.box, None, None)
        n = _norm_idx(idx, self.shape)
        box = list(self.box)
        newbase = []
        newshape = []
        for (lo, hi, drop), (md, off) in zip(n, self.base):
            box[md] = (off + lo, off + hi)
            if not drop:
                newbase.append((md, off + lo))
                newshape.append(hi - lo)
        return V(self.ap[idx], self.mem, tuple(box), tuple(newshape), newbase)

    def re(self, ap):
        return V(ap, self.mem, self.box, tuple(ap.shape), None)


def mkview(ap, mem, shape):
    return V(ap, mem, tuple((0, s) for s in shape), tuple(shape), [(i, 0) for i in range(len(shape))])


class Sched:
    SAME_ENGINE_SYNC = True
    EPOCH = 30000
    NQ = 8

    def __init__(self, nc, es):
        self.nc = nc
        self.es = es
        self.E = {"pe": nc.tensor, "dve": nc.vector, "act": nc.scalar, "pool": nc.gpsimd, "sp": nc.sync}
        self.semobj = {}
        self.nsem = 0
        self.cur = {}
        self.cnt = {}
        self.mysems = {e: set() for e in self.E}
        for e in self.E:
            self._new_epoch(e)
        self.waited = {e: {} for e in self.E}
        self.dq = {}
        for q in ("sp", "pool", "act"):
            sems = [self._sem(f"dq_{q}{i}") for i in range(self.NQ)]
            self.dq[q] = {"sems": sems, "i": 0}
        self.nops = 0
        self.nwaits = 0

    def _sem(self, name):
        s = self.es.enter_context(self.nc.semaphore(name))
        sid = self.nsem
        self.nsem += 1
        self.semobj[sid] = s
        return sid

    def _new_epoch(self, e):
        sid = self._sem(f"e_{e}{self.nsem}")
        self.cur[e] = sid
        self.cnt[e] = 0
        self.mysems[e].add(sid)

    def _deps(self, reads, writes):
        d = {}
        for v in reads:
            m = v.mem
            _merge(d, m.birth)
            for box, w, r in m.recs:
                if _ovl(box, v.box):
                    _merge(d, w)
                    if m.excl:
                        _merge(d, r)
        for v in writes:
            m = v.mem
            _merge(d, m.birth)
            for box, w, r in m.recs:
                if _ovl(box, v.box):
                    _merge(d, w)
                    _merge(d, r)
        return d

    def _record(self, reads, writes, tok):
        sid, val = tok
        for v in reads:
            for rec in v.mem.recs:
                if _ovl(rec[0], v.box):
                    if rec[2].get(sid, 0) < val:
                        rec[2][sid] = val
        for v in writes:
            m = v.mem
            keep = []
            carry_r = {}
            for rec in m.recs:
                if _covers(v.box, rec[0]):
                    continue
                keep.append(rec)
            keep.append([v.box, {sid: val}, {}])
            m.recs = keep

    def _emit_waits(self, eng, deps):
        w = self.waited[eng]
        E = self.E[eng]
        for sid, val in deps.items():
            if sid in self.mysems[eng]:
                if eng == "pe" or eng == "sp" or not self.SAME_ENGINE_SYNC:
                    continue
            if w.get(sid, 0) < val:
                E.wait_ge(self.semobj[sid], val)
                w[sid] = val
                self.nwaits += 1

    def issue(self, eng, fn, reads, writes, signal=True):
        deps = self._deps(reads, writes)
        self._emit_waits(eng, deps)
        inst = fn()
        self.nops += 1
        sid = self.cur[eng]
        if signal:
            self.cnt[eng] += 1
            inst.then_inc(self.semobj[sid], 1)
            tok = (sid, self.cnt[eng])
        else:
            tok = (sid, self.cnt[eng] + 1)
        self._record(reads, writes, tok)
        if signal and self.cnt[eng] >= self.EPOCH:
            self._new_epoch(eng)
        return tok

    def dma(self, q, out, in_, **kw):
        deps = self._deps([in_], [out])
        D = self.dq[q]
        i = D["i"]
        D["i"] += 1
        sid = D["sems"][i % self.NQ]
        rnd = i // self.NQ
        if rnd > 0:
            deps_prev = {sid: 16 * rnd}
            _merge(deps, deps_prev)
        self._emit_waits(q, deps)
        inst = self.E[q].dma_start(out=out.ap, in_=in_.ap, **kw)
        inst.then_inc(self.semobj[sid], 16)
        tok = (sid, 16 * (rnd + 1))
        self._record([in_], [out], tok)
        self.nops += 1
        return tok

    def wait_all(self, eng, mems):
        d = {}
        for m in mems:
            _merge(d, m.summary())
        self._emit_waits(eng, d)

    def mm(self, out, lhsT, rhs, start=True, stop=True, signal=None, **kw):
        if signal is None:
            signal = stop
        return self.issue("pe", lambda: self.nc.tensor.matmul(out.ap, lhsT=lhsT.ap, rhs=rhs.ap, start=start, stop=stop, **kw),
                          [lhsT, rhs], [out], signal=signal)

    def transpose(self, out, in_, ident, signal=True):
        return self.issue("pe", lambda: self.nc.tensor.transpose(out=out.ap, in_=in_.ap, identity=ident.ap),
                          [in_, ident], [out], signal=signal)

    def act(self, out, in_, func, bias=None, scale=None, eng="act"):
        reads = [in_]
        kw = {}
        if bias is not None:
            if isinstance(bias, V):
                reads.append(bias); kw["bias"] = bias.ap
            else:
                kw["bias"] = float(bias)
        if scale is not None:
            if isinstance(scale, V):
                reads.append(scale); kw["scale"] = scale.ap
            else:
                kw["scale"] = float(scale)
        return self.issue("act", lambda: self.nc.scalar.activation(out=out.ap, in_=in_.ap, func=func, **kw), reads, [out])

    def tt(self, eng, out, in0, in1, op):
        return self.issue(eng, lambda: self.E[eng].tensor_tensor(out=out.ap, in0=in0.ap, in1=in1.ap, op=op), [in0, in1], [out])

    def ts(self, eng, out, in0, s1, s2, op0, op1=None):
        reads = [in0]
        a1 = s1.ap if isinstance(s1, V) else s1
        a2 = s2.ap if isinstance(s2, V) else s2
        if isinstance(s1, V): reads.append(s1)
        if isinstance(s2, V): reads.append(s2)
        if op1 is None:
            f = lambda: self.E[eng].tensor_scalar(out=out.ap, in0=in0.ap, scalar1=a1, scalar2=None, op0=op0)
        else:
            f = lambda: self.E[eng].tensor_scalar(out=out.ap, in0=in0.ap, scalar1=a1, scalar2=a2, op0=op0, op1=op1)
        return self.issue(eng, f, reads, [out])

    def stt(self, eng, out, in0, scalar, in1, op0, op1):
        reads = [in0, in1]
        a = scalar.ap if isinstance(scalar, V) else scalar
        if isinstance(scalar, V): reads.append(scalar)
        return self.issue(eng, lambda: self.E[eng].scalar_tensor_tensor(out=out.ap, in0=in0.ap, scalar=a, in1=in1.ap, op0=op0, op1=op1),
                          reads, [out])

    def copy(self, eng, out, in_):
        if eng == "act":
            return self.issue("act", lambda: self.nc.scalar.copy(out=out.ap, in_=in_.ap), [in_], [out])
        return self.issue(eng, lambda: self.E[eng].tensor_copy(out=out.ap, in_=in_.ap), [in_], [out])

    def memset(self, eng, out, val):
        return self.issue(eng, lambda: self.E[eng].memset(out.ap, val), [], [out])

    def reduce(self, eng, out, in_, op, axis=AX.X):
        return self.issue(eng, lambda: self.E[eng].tensor_reduce(out=out.ap, in_=in_.ap, axis=axis, op=op), [in_], [out])

    def recip(self, out, in_):
        return self.issue("dve", lambda: self.nc.vector.reciprocal(out=out.ap, in_=in_.ap), [in_], [out])


class Arena:
    def __init__(self, sched, ap_f32, nwords):
        self.s = sched
        self.ap = ap_f32
        self.n = nwords
        self.top = 0
        self.dead = []
        self.live = []

    def alloc(self, name, shape, dtype, p=128):
        isz = _ISZ[dtype]
        free = int(np.prod(shape[1:]))
        nbytes = free * isz
        nw = (nbytes + 3) // 4
        lo, hi = self.top, self.top + nw
        assert hi <= self.n, f"arena overflow allocating {name}: {hi} > {self.n}"
        self.top = hi
        birth = {}
        nd = []
        for (dl, dh, toks) in self.dead:
            if dl < hi and lo < dh:
                _merge(birth, toks)
                if lo <= dl and dh <= hi:
                    continue
            nd.append((dl, dh, toks))
        self.dead = nd
        mem = Mem(name, birth)
        ap = self.ap[0:shape[0], lo:hi]
        if dtype != F32:
            ap = ap.bitcast(dtype)
        ap = ap[:, 0:free]
        if len(shape) > 2:
            names = " ".join(f"d{i}" for i in range(1, len(shape)))
            kw = {f"d{i}": shape[i] for i in range(2, len(shape))}
            ap = ap.rearrange(f"p ({names}) -> p {names}", **kw)
        self.live.append((lo, hi, mem))
        return mkview(ap, mem, shape)

    def mark(self):
        return (self.top, len(self.live))

    def release(self, mark):
        top, nl = mark
        while len(self.live) > nl:
            lo, hi, mem = self.live.pop()
            self.dead.append((lo, hi, mem.summary()))
        self.top = top

    @contextlib.contextmanager
    def scope(self):
        m = self.mark()
        yield
        self.release(m)


D = 1024; BW = 512; KS_RET = 128 ** -0.5; ATT_SCALE = 64 ** -0.5; EPS = 1e-6
IN_OFF = {}
_o = 0
for _n, _s in zip(('ret_q','ret_k','ret_v','ret_z','sc_b','sc_c','sc_x','sc_z','cf_a','cf_g','cf_z','att_q','att_k','att_v','att_z','gates'),
                  (512,512,512,512,512,512,512,512,512,512,512,512,128,128,512,4096)):
    IN_OFF[_n] = _o; _o += _s
IN_TOTAL = _o
NGRAN = 23
G_RQ, G_RK, G_RV, G_RZ, G_SB, G_SC, G_SX, G_SZ, G_CA, G_CG, G_CZ, G_AQ, G_AQS, G_AKV, G_AZ, G_GATE = 0,1,2,3,4,5,6,7,8,9,10,11,12,13,14,15


def par_layout(DEPTH):
    lay = {}; off = 0
    for name, shp in (("b_mod", (DEPTH, 24)), ("norm_w", (DEPTH, 8)), ("fnw", (8,)), ("gn_w", (DEPTH, 4)), ("scw", (DEPTH, 4, 3)),
                      ("cfw", (DEPTH, 4, 31)), ("cfb", (DEPTH, 4)), ("lnw", (DEPTH, 4)), ("lnb", (DEPTH, 4)),
                      ("rdec", (DEPTH, 8)), ("sink", (DEPTH, 8)), ("cvec", (8, 2))):
        n = int(np.prod(shp)); lay[name] = (off, shp); off += n
    return lay, off

CON_NAMES = ("ident", "posdF", "posdB", "MF", "MB", "rampF1", "rampB", "c128", "onesf", "mprev", "mnext")

def con_layout():
    lay = {}; off = 0
    for nm in CON_NAMES:
        lay[nm] = off; off += 128
    lay["rp127"] = off; off += 1
    lay["rp"] = off; off += 1
    return lay, off


def host_consts(S, L):
    lay, n = con_layout()
    C = np.zeros((128, n), np.float32)
    j = np.arange(128)[:, None].astype(np.float32); i = np.arange(128)[None, :].astype(np.float32)
    def put(nm, a): C[:, lay[nm]:lay[nm] + 128] = a
    put("ident", np.eye(128)); put("posdF", np.maximum(i - j, 0)); put("posdB", np.maximum(j - i, 0))
    put("MF", (i >= j)); put("MB", (j > i)); put("rampF1", np.broadcast_to(i + 1, (128, 128))); put("rampB", np.broadcast_to(128 - i, (128, 128)))
    put("c128", np.full((128, 128), 128.0)); put("onesf", np.ones((128, 128)))
    put("mprev", (j >= i)); put("mnext", (j <= i))
    C[:, lay["rp127"]] = 127 - np.arange(128); C[:, lay["rp"]] = np.arange(128)
    NT = L + S
    t = np.arange(S); row = (t // 64).astype(np.float32); col = (t % 64).astype(np.float32)
    inv = (10000.0 ** (-np.arange(16, dtype=np.float32) / 16)).astype(np.float32)
    ang = np.concatenate([row[:, None] * inv[None], col[:, None] * inv[None]], -1).astype(np.float32)
    cos = np.cos(ang).astype(np.float32); sin = np.sin(ang).astype(np.float32)
    R = np.zeros((2, 128, NT), np.float32); R[0, :, :L] = 1.0
    for p in range(128):
        d = p % 64
        R[0, p, L:] = cos[:, d % 32]
        R[1, p, L:] = (-sin[:, d % 32]) if d < 32 else sin[:, d % 32]
    return C, R


def host_params(inp, b, DEPTH):
    lay, n = par_layout(DEPTH)
    P = np.zeros((128, n), np.float32)
    def put(nm, arr):
        off, shp = lay[nm]; P[:, off:off + int(np.prod(shp))] = arr.reshape(128, -1)
    fm = lambda a, k: np.ascontiguousarray(a.reshape(a.shape[:-1] + (k, 128)))
    put("b_mod", np.moveaxis(fm(inp["b_mod"][:DEPTH], 24), -1, 0))
    put("norm_w", np.moveaxis(fm(inp["norm_w"][:DEPTH], 8), -1, 0))
    put("fnw", np.moveaxis(fm(inp["final_norm_w"], 8), -1, 0))
    put("gn_w", np.moveaxis(fm(inp["ret_gn_w"][:DEPTH], 4), -1, 0))
    put("scw", np.transpose(inp["sc_conv_w"][:DEPTH].reshape(DEPTH, 3, 4, 128), (3, 0, 2, 1)))
    put("cfw", np.transpose(inp["cf_conv_w"][:DEPTH].reshape(DEPTH, 31, 4, 128), (3, 0, 2, 1)))
    put("cfb", np.moveaxis(fm(inp["cf_conv_b"][:DEPTH], 4), -1, 0))
    put("lnw", np.moveaxis(fm(inp["cf_ln_w"][:DEPTH], 4), -1, 0))
    put("lnb", np.moveaxis(fm(inp["cf_ln_b"][:DEPTH], 4), -1, 0))
    put("rdec", np.broadcast_to(inp["ret_decay"][:DEPTH].reshape(1, DEPTH, 8), (128, DEPTH, 8)))
    put("sink", np.broadcast_to(inp["att_sink"][:DEPTH].reshape(1, DEPTH, 8), (128, DEPTH, 8)))
    cv = np.stack([inp["c"][b].reshape(8, 128), inp["c_ctx"].reshape(8, 128)], -1)
    put("cvec", np.transpose(cv, (1, 0, 2)))
    return P


class StopBuild(Exception):
    pass


STQ = "act"


def build_program(S, L, DEPTH, TB=512, STOP=None, DEBUG=False):
    def stage(k):
        if STOP is not None and k == STOP:
            raise StopBuild()
    NT = L + S; NCH = NT // 128; NCC = L // 128
    NTP = NT + 64
    pidx = lambda g: g + 16 if g < L else g + 48
    blocks = [(0, L, True)] + [(L + i * TB, TB, False) for i in range(S // TB)]
    playout, NPAR = par_layout(DEPTH); clayout, NCON = con_layout()
    nc = bass.Bass("TRN2", target_bir_lowering=False)
    dt = nc.dram_tensor
    xT_d = dt("xT", [D, S], F32, kind="ExternalInput"); cT_d = dt("cT", [D, L], F32, kind="ExternalInput")
    par_d = dt("par", [128, NPAR], F32, kind="ExternalInput"); con_d = dt("con", [128, NCON], F32, kind="ExternalInput")
    rope_d = dt("rope", [2, 128, NT], F32, kind="ExternalInput")
    wmod_d = dt("w_mod", [DEPTH, D, 3 * D], F32, kind="ExternalInput")
    win_d = dt("w_in", [DEPTH, D, IN_TOTAL], F32, kind="ExternalInput")
    wbr_d = dt("w_branch", [DEPTH, 4, BW, D], F32, kind="ExternalInput")
    wout_d = dt("w_out", [DEPTH, D, D], F32, kind="ExternalInput")
    yT_d = dt("yT", [D, S], F32, kind="ExternalOutput")
    XS_d = [dt(f"XS{i}", [D, NT], F32, kind="Internal") for i in range(2)]
    WS_d = [dt(f"WS{l}", [D, NGRAN * 512], BF16, kind="Internal") for l in range(DEPTH)]
    WB_d = [dt(f"WB{l}", [4, BW, D], BF16, kind="Internal") for l in range(DEPTH)]
    WO_d = [dt(f"WO{l}", [D, D], BF16, kind="Internal") for l in range(DEPTH)]
    CD_d = [dt(f"CD{l}", [5, 128, 31 * 128], BF16, kind="Internal") for l in range(DEPTH)]
    dk = "ExternalOutput" if DEBUG else "Internal"
    SF_d = dt("SFs", [NCH, 128, 512], BF16, kind=dk); SB_d = dt("SBs", [NCH, 128, 512], BF16, kind=dk)
    USC_d = dt("USC", [BW, NTP], BF16, kind=dk); UCF_d = dt("UCF", [BW, NTP], BF16, kind=dk)
    DBG_d = dt("DBG", [4, BW, NT], BF16, kind=dk) if DEBUG else None

    es = contextlib.ExitStack()
    with es:
        es.push(lambda et, ev, tb: et is not None and issubclass(et, StopBuild))
        es.enter_context(nc.allow_low_precision("bf16 matmul operands, fp32 accumulation"))
        es.enter_context(nc.allow_non_contiguous_dma("small strided weight gathers"))
        S_ = Sched(nc, es)
        NW = 52800
        ar_t = es.enter_context(nc.sbuf_tensor("arena", [128, NW], F32))
        AR = Arena(S_, ar_t[:, :], NW)
        psb = [es.enter_context(nc.psum_tensor(f"psb{i}", [128, 512], F32)) for i in range(8)]
        psm = [Mem(f"ps{i}", excl=True) for i in range(8)]
        pstate = {"i": 0}

        def pv(bank, dtype=F32, shape=(128, 512)):
            n = int(np.prod(shape[1:]))
            if dtype == F32:
                ap = psb[bank][:, 0:n]
            else:
                ap = psb[bank][:, 0:(n + 1) // 2].bitcast(BF16)[:, 0:n]
            if len(shape) > 2:
                names = " ".join(f"d{i}" for i in range(1, len(shape)))
                kw = {f"d{i}": shape[i] for i in range(2, len(shape))}
                ap = ap.rearrange(f"p ({names}) -> p {names}", **kw)
            return V(ap, psm[bank], ((0, 1),), None, None)

        def ps(dtype=F32, shape=(128, 512)):
            b = pstate["i"] % 6; pstate["i"] += 1
            return pv(b, dtype, shape)

        def dview(t, ap, box):
            return V(ap, dmem[t.name], box, None, None)
        dmem = {}
        for t in [xT_d, cT_d, par_d, con_d, rope_d, wmod_d, win_d, wbr_d, wout_d, yT_d, SF_d, SB_d, USC_d, UCF_d] + XS_d + WS_d + WB_d + WO_d + CD_d:
            dmem[t.name] = Mem(t.name)

        PAR = AR.alloc("PAR", (128, NPAR), F32); CON = AR.alloc("CON", (128, NCON), F32)
        S_.dma("sp", PAR, dview(par_d, par_d.ap(), ((0, 128), (0, NPAR))))
        S_.dma("sp", CON, dview(con_d, con_d.ap(), ((0, 128), (0, NCON))))
        def par(nm, *idx):
            off, shp = playout[nm]; n = int(np.prod(shp))
            v = PAR[:, off:off + n]
            if len(shp) > 1:
                names = " ".join(f"d{i}" for i in range(len(shp)))
                kw = {f"d{i}": shp[i] for i in range(1, len(shp))}
                v = v.re(v.ap.rearrange(f"p ({names}) -> p {names}", **kw))
            return v[(slice(None),) + idx] if idx else v
        def con(nm, w=128):
            o = clayout[nm]; return CON[:, o:o + w]
        HX = AR.alloc("HX", (128, 8, NT), BF16)
        KR = AR.alloc("KR", (128, NT), BF16)
        VA = AR.alloc("VA", (128, NCH, 2, 65), BF16)
        QA = AR.alloc("QA", (128, 4, TB), BF16); QB = AR.alloc("QB", (128, 4, TB), BF16)
        NWR = 4
        WR = [AR.alloc(f"WR{i}", (128, 4096), BF16) for i in range(NWR)]
        wst = {"i": 0}
        IDB = AR.alloc("IDB", (128, 128), BF16); ONESD = AR.alloc("ONESD", (128, 128), BF16); ONESC = AR.alloc("ONESC", (128, 128), BF16)
        MPREV = AR.alloc("MPREV", (128, 128), BF16); MNEXT = AR.alloc("MNEXT", (128, 128), BF16)
        MOD = AR.alloc("MOD", (128, DEPTH, 24, 2), F32)
        AXC = AR.alloc("AXC", (128, DEPTH, 2, 8), F32)
        LG = AR.alloc("LG", (128, 8), F32); ESINK = AR.alloc("ESINK", (128, 8), F32)
        DT = AR.alloc("DT", (128, 4, 128), F32); QWF = AR.alloc("QWF", (128, 4, 128), F32); QWB = AR.alloc("QWB", (128, 4, 128), F32)
        KWF = AR.alloc("KWF", (128, 512), F32); KWB = AR.alloc("KWB", (128, 512), F32)
        GF = AR.alloc("GF", (128, 512), F32); GB = AR.alloc("GB", (128, 512), F32)
        SMALL = AR.alloc("SMALL", (128, 64), F32)
        S_.copy("dve", IDB, con("ident")); S_.memset("dve", ONESD, 1.0 / 1024); S_.memset("dve", ONESC, 1.0 / 512)
        S_.copy("dve", MPREV, con("mprev")); S_.copy("dve", MNEXT, con("mnext"))
        S_.memset("pool", VA, 1.0); S_.memset("pool", QA, 0.0); S_.memset("pool", QB, 0.0)

        def wload(src_view, shape3):
            slot = WR[wst["i"] % NWR]; wst["i"] += 1
            a, b = shape3
            v = slot[:, 0:a * b]
            v = v.re(v.ap.rearrange("p (a b) -> p a b", b=b))
            S_.dma("sp", v, src_view)
            return v

        def ws_gran(l, g):
            ap = WS_d[l].ap().rearrange("(kt p) n -> p kt n", p=128)[:, :, g * 512:(g + 1) * 512]
            return dview(WS_d[l], ap, ((0, D), (g * 512, (g + 1) * 512)))

        with AR.scope():
            Z = AR.alloc("Z", (128, 4, 32), BF16)
            S_.memset("dve", Z, 0.0)
            for U_d in (USC_d, UCF_d):
                for (lo, w) in ((0, 16), (16 + L, 32), (NTP - 16, 16)):
                    ap = U_d.ap().rearrange("(c p) t -> p c t", p=128)[:, :, lo:lo + w]
                    S_.dma("sp", dview(U_d, ap, ((0, BW), (lo, lo + w))), Z[:, :, 0:w])
        def xblock_src(l, g0, n, isc):
            if l == 0:
                t, o = (cT_d, g0) if isc else (xT_d, g0 - L)
            else:
                t, o = XS_d[l % 2], g0
            return dview(t, t.ap().rearrange("(j p) t -> p j t", p=128)[:, :, o:o + n], ((0, D), (o, o + n)))

        def xchunk_src(l, oc, g0, n, isc):
            if l == 0:
                t, o = (cT_d, g0) if isc else (xT_d, g0 - L)
            else:
                t, o = XS_d[l % 2], g0
            return dview(t, t.ap()[oc * 128:(oc + 1) * 128, o:o + n], ((oc * 128, oc * 128 + 128), (o, o + n)))

        def cast(dst_t, dst_ap, dbox, src_t, src_ap):
            S_.dma("pool", dview(dst_t, dst_ap, dbox), dview(src_t, src_ap, ((0, 1),)))

        def cast_layer(l):
            W = win_d.ap()[l]; Wd = WS_d[l].ap()
            def seg(gc, c0, n, src0):
                d0 = gc
                cast(WS_d[l], Wd[:, d0:d0 + n], ((0, D), (d0, d0 + n)), win_d, W[:, src0:src0 + n])
            names = ['ret_q','ret_k','ret_v','ret_z','sc_b','sc_c','sc_x','sc_z','cf_a','cf_g','cf_z']
            for gi in (1, 2, 5, 6, 8, 9):
                seg(gi * 512, 0, 512, gi * 512)
            aq = IN_OFF['att_q']
            for m in range(4):
                for hh, h in enumerate((m, 4 + m)):
                    seg(G_AQ * 512 + m * 128 + hh * 64, 0, 64, aq + h * 64)
                    seg(G_AQS * 512 + m * 128 + hh * 64, 0, 32, aq + h * 64 + 32)
                    seg(G_AQS * 512 + m * 128 + hh * 64 + 32, 0, 32, aq + h * 64)
            ak = IN_OFF['att_k']; av = IN_OFF['att_v']
            seg(G_AKV * 512, 0, 128, ak)
            for g in range(2):
                seg(G_AKV * 512 + 128 + g * 64, 0, 32, ak + g * 64 + 32)
                seg(G_AKV * 512 + 128 + g * 64 + 32, 0, 32, ak + g * 64)
            seg(G_AKV * 512 + 256, 0, 128, av)
            seg(G_AKV * 512 + 384, 0, 128, av)
            for gi in (0, 3, 4, 7, 10):
                seg(gi * 512, 0, 512, gi * 512)
            seg(G_AZ * 512, 0, 512, IN_OFF['att_z'])
            seg(G_GATE * 512, 0, 4096, IN_OFF['gates'])
            cast(WB_d[l], WB_d[l].ap(), ((0, 4), (0, BW), (0, D)), wbr_d, wbr_d.ap()[l])
            cast(WO_d[l], WO_d[l].ap(), ((0, D), (0, D)), wout_d, wout_d.ap()[l])

        stage(0)

        SC = AR.alloc("SC", (128, 8, 2), BF16)
        S_.act(SC, par("cvec"), AF.Silu)

        def do_mod(l):
            for jc in range(6):
                slot = WR[wst["i"] % NWR]; wst["i"] += 1
                wv = slot.re(slot.ap.rearrange("p (a b) -> p a b", b=512))
                src = wmod_d.ap()[l].rearrange("(kt p) n -> p kt n", p=128)[:, :, jc * 512:(jc + 1) * 512]
                S_.dma("pool", wv, dview(wmod_d, src, ((0, 1),)))
                P = ps()
                for j4 in range(4):
                    for kt in range(8):
                        S_.mm(P[:, j4 * 2:j4 * 2 + 2], wv[:, kt, j4 * 128:(j4 + 1) * 128], SC[:, kt, :], start=(kt == 0), stop=(kt == 7))
                for j4 in range(4):
                    j = jc * 4 + j4
                    S_.ts("dve", MOD[:, l, j, :], P[:, j4 * 2:j4 * 2 + 2], par("b_mod", l, slice(j, j + 1)), None, ALU.add)
            for w in range(2):
                S_.stt("dve", AXC[:, l, w, :], MOD[:, l, 8:16, w], 1.0, par("norm_w", l), ALU.add, ALU.mult)

        def do_diag(l):
            saved = AR.top
            AR.top = NW - 1984
            m = AR.mark()
            stg_ = [AR.alloc("stg0", (128, 31, 128), BF16)]
            for gi in range(5):
                stg = stg_[0]
                nt = 31 if gi < 4 else 12
                for k in range(nt):
                    wcol = par("cfw", l, gi, slice(k, k + 1)) if gi < 4 else par("scw", l, k // 3, slice(k % 3, k % 3 + 1))
                    S_.ts("pool", stg[:, k, :], con("ident"), wcol, None, ALU.mult)
                ap = CD_d[l].ap()[gi].rearrange("p (k q) -> p k q", q=128)[:, 0:nt, :]
                S_.dma("pool", dview(CD_d[l], ap, ((gi, gi + 1),)), stg[:, 0:nt, :])
            AR.release(m)
            AR.top = saved

        do_mod(0)
        stage(1)
        for l in range(DEPTH):
            cast_layer(l)
            do_diag(l)
        stage(3)

        def nscope(name):
            return nc.named_scope(name) if DEBUG else contextlib.nullcontext()

        def Avec(l, w, j): return AXC[:, l, w, j:j + 1]
        def Bvec(l, w, j): return MOD[:, l, j, w:w + 1]
        def Gvec(l, w, j): return MOD[:, l, 16 + j, w:w + 1]

        def proj(wv, c0, blk, n, out=None):
            P = out if out is not None else ps()
            for kt in range(8):
                S_.mm(P[:, 0:n], wv[:, kt, c0:c0 + 128], HX[:, kt, blk], start=(kt == 0), stop=(kt == 7))
            return P

        for l in range(DEPTH):
            last = (l == DEPTH - 1)
            cur, nxt = XS_d[l % 2], XS_d[(l + 1) % 2]
            with AR.scope():
                t0 = AR.alloc("t0", (128, 8), F32); e1 = AR.alloc("e1", (128, 128), F32); e2 = AR.alloc("e2", (128, 128), F32)
                kw1 = AR.alloc("kw1", (128, 8), F32)
                S_.act(t0, par("rdec", l), AF.Exp, scale=-1.0)
                S_.act(t0, t0, AF.Ln, bias=1.0)
                S_.ts("dve", LG, t0, -1.0, None, ALU.mult)
                S_.act(ESINK, par("sink", l), AF.Exp)
                for h in range(4):
                    S_.act(e1, con("posdF"), AF.Exp, scale=LG[:, h:h + 1])
                    S_.tt("dve", e1, e1, con("MF"), ALU.mult)
                    S_.act(e2, con("posdB"), AF.Exp, scale=LG[:, 4 + h:5 + h])
                    S_.tt("dve", e2, e2, con("MB"), ALU.mult)
                    S_.tt("dve", e1, e1, e2, ALU.add)
                    S_.ts("dve", DT[:, h, :], e1, KS_RET, None, ALU.mult)
                    S_.act(QWF[:, h, :], con("rampF1"), AF.Exp, scale=LG[:, h:h + 1])
                    S_.act(QWB[:, h, :], con("rampB"), AF.Exp, scale=LG[:, 4 + h:5 + h])
                    S_.act(kw1[:, h:h + 1], con("rp127", 1), AF.Exp, scale=LG[:, h:h + 1])
                    S_.act(kw1[:, 4 + h:5 + h], con("rp", 1), AF.Exp, scale=LG[:, 4 + h:5 + h])
                    S_.ts("dve", KWF[:, h * 128:(h + 1) * 128], con("onesf"), kw1[:, h:h + 1], KS_RET, ALU.mult, ALU.mult)
                    S_.ts("dve", KWB[:, h * 128:(h + 1) * 128], con("onesf"), kw1[:, 4 + h:5 + h], KS_RET, ALU.mult, ALU.mult)
                    S_.act(GF[:, h * 128:(h + 1) * 128], con("c128"), AF.Exp, scale=LG[:, h:h + 1])
                    S_.act(GB[:, h * 128:(h + 1) * 128], con("c128"), AF.Exp, scale=LG[:, 4 + h:5 + h])
            stage(10 + 100 * l)
            with AR.scope():
                sets = []
                sqw = AR.alloc("sqw", (128, 8, TB), BF16)
                for k in range(2):
                    sets.append(dict(xb=AR.alloc(f"xb{k}", (128, 8, TB), F32), rstd=AR.alloc(f"rstd{k}", (128, TB), F32),
                                     tmp=[AR.alloc(f"tmp{k}{q}", (128, TB), F32) for q in range(1)]))
                for bi_, (g0, n, isc) in enumerate(blocks):
                    w = 1 if isc else 0
                    B_ = sets[bi_ % 2]
                    with nscope(f"p1_L{l}_{g0}"):
                        xb = B_["xb"][:, :, 0:n]; rstd = B_["rstd"][:, 0:n]; sq = sqw[:, :, 0:n]
                        S_.dma("sp", xb, xblock_src(l, g0, n, isc))
                        S_.act(sq, xb, AF.Square)
                        P = ps()
                        for j in range(8):
                            S_.mm(P[:, 0:n], ONESD, sq[:, j, :], start=(j == 0), stop=(j == 7))
                        S_.act(rstd, P[:, 0:n], AF.Sqrt, bias=EPS)
                        S_.recip(rstd, rstd)
                        for j in range(8):
                            tmp = B_["tmp"][0][:, 0:n]
                            S_.stt("dve", tmp, xb[:, j, :], Avec(l, w, j), rstd, ALU.mult, ALU.mult)
                            S_.ts("dve", HX[:, j, g0:g0 + n], tmp, Bvec(l, w, j), None, ALU.add)

            stage(11 + 100 * l)
            fwd = list(range(NCH))
            bwd = list(range(NCC - 1, -1, -1)) + list(range(NCH - 1, NCC - 1, -1))
            for (order, KW_, G_, ST_d) in ((fwd, KWF, GF, SF_d), (bwd, KWB, GB, SB_d)):
                with AR.scope(), nscope(f"p2_L{l}_{ST_d.name}"):
                    Wk = wload(ws_gran(l, G_RK), (8, 512)); Wv = wload(ws_gran(l, G_RV), (8, 512))
                    ST = AR.alloc("ST", (128, 512), F32); S_.memset("dve", ST, 0.0)
                    tm = AR.alloc("tm", (128, 512), F32)
                    kwb_ = [AR.alloc(f"kw{i}", (128, 512), BF16) for i in range(2)]
                    vb_ = [AR.alloc(f"vb{i}", (128, 512), BF16) for i in range(2)]
                    sb_ = [AR.alloc(f"sb{i}", (128, 512), BF16) for i in range(2)]
                    for ci, c in enumerate(order):
                        kp = ps(); vp = ps()
                        for kt in range(8):
                            S_.mm(kp, HX[:, kt, c * 128:(c + 1) * 128], Wk[:, kt, :], start=(kt == 0), stop=(kt == 7))
                        for kt in range(8):
                            S_.mm(vp, HX[:, kt, c * 128:(c + 1) * 128], Wv[:, kt, :], start=(kt == 0), stop=(kt == 7))
                        kw = kwb_[ci % 2]; vb = vb_[ci % 2]; sb = sb_[ci % 2]
                        S_.tt("dve", kw, kp, KW_, ALU.mult)
                        S_.copy("act", vb, vp)
                        kv = ps()
                        for h in range(4):
                            S_.mm(kv[:, h * 128:(h + 1) * 128], kw[:, h * 128:(h + 1) * 128], vb[:, h * 128:(h + 1) * 128])
                        S_.copy("act", sb, ST)
                        S_.dma("sp", dview(ST_d, ST_d.ap()[c], ((c, c + 1),)), sb)
                        S_.tt("dve", tm, ST, G_, ALU.mult)
                        S_.tt("dve", ST, tm, kv, ALU.add)

            stage(12 + 100 * l)
            with AR.scope(), nscope(f"p2b_L{l}"):
                Wc = wload(ws_gran(l, G_SC), (8, 512)); Wx = wload(ws_gran(l, G_SX), (8, 512))
                Wa = wload(ws_gran(l, G_CA), (8, 512)); Wg = wload(ws_gran(l, G_CG), (8, 512))
                for (g0, n, isc) in blocks:
                    blk = slice(g0, g0 + n)
                    with AR.scope():
                        usc = AR.alloc("usc", (128, 4, n), BF16); ucf = AR.alloc("ucf", (128, 4, n), BF16)
                        xs_ = [AR.alloc(f"xs{i}", (128, n), F32) for i in range(2)]
                        for ch in range(4):
                            pc = proj(Wc, ch * 128, blk, n); px = proj(Wx, ch * 128, blk, n)
                            t = xs_[0]
                            S_.copy("act", t, px[:, 0:n])
                            S_.tt("dve", usc[:, ch, :], pc[:, 0:n], t, ALU.mult)
                            pa = proj(Wa, ch * 128, blk, n); pg = proj(Wg, ch * 128, blk, n)
                            t = xs_[1]
                            S_.act(t, pg[:, 0:n], AF.Sigmoid)
                            S_.tt("dve", ucf[:, ch, :], pa[:, 0:n], t, ALU.mult)
                        p0 = pidx(g0)
                        for U_d, u in ((USC_d, usc), (UCF_d, ucf)):
                            ap = U_d.ap().rearrange("(c p) t -> p c t", p=128)[:, :, p0:p0 + n]
                            S_.dma("sp", dview(U_d, ap, ((0, BW), (p0, p0 + n))), u)

            stage(13 + 100 * l)
            with AR.scope(), nscope(f"p2c_L{l}"):
                Wkv = wload(ws_gran(l, G_AKV), (8, 512))
                for (g0, n, isc) in blocks:
                    blk = slice(g0, g0 + n)
                    with AR.scope():
                        cs = AR.alloc("cs", (128, 2, n), F32); t1 = AR.alloc("t1", (128, n), F32); t2 = AR.alloc("t2", (128, n), F32)
                        src = rope_d.ap().rearrange("a p t -> p a t")[:, :, g0:g0 + n]
                        S_.dma("sp", cs, dview(rope_d, src, ((0, 1),)))
                        pk = proj(Wkv, 0, blk, n); pks = proj(Wkv, 128, blk, n)
                        S_.tt("dve", t1, pk[:, 0:n], cs[:, 0, :], ALU.mult)
                        S_.tt("dve", t2, pks[:, 0:n], cs[:, 1, :], ALU.mult)
                        S_.tt("dve", KR[:, blk], t1, t2, ALU.add)
                        for c in range(g0 // 128, (g0 + n) // 128):
                            pvv = ps()
                            for kt in range(8):
                                S_.mm(pvv[:, 0:128], HX[:, kt, c * 128:(c + 1) * 128], Wkv[:, kt, 256:384], start=(kt == 0), stop=(kt == 7))
                            src3 = pvv[:, 0:128]; src3 = src3.re(src3.ap.rearrange("p (g e) -> p g e", e=64))
                            S_.copy("act", VA[:, c, :, 0:64], src3)

            if l + 1 < DEPTH:
                do_mod(l + 1)
            stage(14 + 100 * l)
            for (g0, n, isc) in blocks:
                if last and isc:
                    continue
                w = 1 if isc else 0
                blk = slice(g0, g0 + n); nchk = n // 128; c0 = g0 // 128
                with AR.scope():
                    BR = AR.alloc("BR", (128, 4, n), BF16)
                    accbox = {}

                    def merge(i):
                        if DEBUG:
                            if "DBG" not in dmem: dmem["DBG"] = Mem("DBG")
                            S_.dma("sp", dview(DBG_d, DBG_d.ap()[i].rearrange("(c p) t -> p c t", p=128)[:, :, g0:g0 + n], ((i, i + 1), (0, BW), (g0, g0 + n))), BR)
                        Wb = wload(dview(WB_d[l], WB_d[l].ap()[i].rearrange("(kt p) n -> p kt n", p=128), ((i, i + 1),)), (4, 1024))
                        with AR.scope(), nscope(f"mrg{i}_L{l}_{g0}"):
                            sg_ = [AR.alloc(f"sg{k}", (128, n), F32) for k in range(2)]
                            tt_ = [AR.alloc(f"tt{k}", (128, n), F32) for k in range(2)]
                            for half in range(2):
                                Wg_ = wload(ws_gran(l, G_GATE + 2 * i + half), (8, 512))
                                for o4 in range(4):
                                    oc = half * 4 + o4
                                    Gp = proj(Wg_, o4 * 128, blk, n)
                                    Pp = ps()
                                    for kt in range(4):
                                        S_.mm(Pp[:, 0:n], Wb[:, kt, oc * 128:(oc + 1) * 128], BR[:, kt, :], start=(kt == 0), stop=(kt == 3))
                                    sg = sg_[oc % 2]; tq = tt_[oc % 2]
                                    S_.act(sg, Gp[:, 0:n], AF.Sigmoid)
                                    if i == 0:
                                        S_.tt("dve", accbox['a'][:, oc, :], Pp[:, 0:n], sg, ALU.mult)
                                    else:
                                        S_.tt("dve", tq, Pp[:, 0:n], sg, ALU.mult)
                                        dst = accbox['m'][:, oc, :] if i == 3 else accbox['a'][:, oc, :]
                                        S_.tt("dve", dst, accbox['a'][:, oc, :], tq, ALU.add)
                                        if i == 3 and l >= 1:
                                            S_.dma("pool", accbox['a'][:, oc, :], xchunk_src(l, oc, g0, n, isc))


                    def tok2feat(on, zs, col0):
                        T = ps(BF16, (128, 4, 128))
                        for k in range(4):
                            S_.transpose(T[:, k, :], on[:, k * 128:(k + 1) * 128], IDB, signal=(k == 3))
                        S_.tt("dve", BR[:, :, col0:col0 + 128], T, zs[:, :, col0:col0 + 128], ALU.mult)

                    with AR.scope(), nscope(f"b3a_L{l}_{g0}"):
                        Wq = wload(ws_gran(l, G_RQ), (8, 512)); Wk = wload(ws_gran(l, G_RK), (8, 512))
                        Wz = wload(ws_gran(l, G_RZ), (8, 512)); Wv = wload(ws_gran(l, G_RV), (8, 512))
                        qT = AR.alloc("qT", (128, 4, n), BF16); kT = AR.alloc("kT", (128, 4, n), BF16)
                        zs = AR.alloc("zs", (128, 4, n), BF16); vt = AR.alloc("vt", (128, nchk, 512), BF16)
                        SFb = AR.alloc("SFb", (128, nchk, 512), BF16); SBb = AR.alloc("SBb", (128, nchk, 512), BF16)
                        for ST_d, dst in ((SF_d, SFb), (SB_d, SBb)):
                            S_.dma("sp", dst, dview(ST_d, ST_d.ap()[c0:c0 + nchk].rearrange("c p f -> p c f"), ((c0, c0 + nchk),)))
                        for h in range(4):
                            P = proj(Wq, h * 128, blk, n); S_.copy("act", qT[:, h, :], P[:, 0:n])
                        for h in range(4):
                            P = proj(Wk, h * 128, blk, n); S_.copy("act", kT[:, h, :], P[:, 0:n])
                        for h in range(4):
                            P = proj(Wz, h * 128, blk, n); S_.act(zs[:, h, :], P[:, 0:n], AF.Silu)
                            S_.ts("dve", zs[:, h, :], zs[:, h, :], par("gn_w", l, slice(h, h + 1)), None, ALU.mult)
                        for ci in range(nchk):
                            P = ps()
                            for kt in range(8):
                                S_.mm(P, HX[:, kt, g0 + ci * 128:g0 + (ci + 1) * 128], Wv[:, kt, :], start=(kt == 0), stop=(kt == 7))
                            S_.copy("act", vt[:, ci, :], P)
                        stage(15 + 100 * l)
                        sTm_ = [AR.alloc(f"sTm{k}", (128, 4, 128), BF16) for k in range(2)]
                        qf_ = [AR.alloc(f"qf{k}", (128, 4, 128), BF16) for k in range(2)]
                        qb_ = [AR.alloc(f"qb{k}", (128, 4, 128), BF16) for k in range(2)]
                        on_ = [AR.alloc(f"on{k}", (128, 512), BF16) for k in range(2)]
                        sqt = AR.alloc("sqt", (128, 4, 128), F32)
                        st_ = [AR.alloc(f"st{k}", (128, 16), F32) for k in range(2)]
                        def stA(ci):
                            cs_ = slice(ci * 128, (ci + 1) * 128)
                            sTm = sTm_[ci % 2]; qf = qf_[ci % 2]; qb = qb_[ci % 2]
                            Ps = ps(F32, (128, 4, 128))
                            for h in range(4):
                                S_.mm(Ps[:, h, :], kT[:, h, cs_], qT[:, h, cs_])
                            S_.tt("dve", sTm, Ps, DT, ALU.mult)
                            S_.tt("dve", qf, qT[:, :, cs_], QWF, ALU.mult)
                            S_.tt("dve", qb, qT[:, :, cs_], QWB, ALU.mult)

                        def stB(ci):
                            sTm = sTm_[ci % 2]; qf = qf_[ci % 2]; qb = qb_[ci % 2]; on = on_[ci % 2]; st = st_[ci % 2]
                            Po = ps(F32, (128, 4, 128))
                            for h in range(4):
                                hs = slice(h * 128, (h + 1) * 128)
                                S_.mm(Po[:, h, :], sTm[:, h, :], vt[:, ci, hs], start=True, stop=False)
                                S_.mm(Po[:, h, :], qf[:, h, :], SFb[:, ci, hs], start=False, stop=False)
                                S_.mm(Po[:, h, :], qb[:, h, :], SBb[:, ci, hs], start=False, stop=True)
                            S_.reduce("dve", st[:, 0:4], Po, ALU.add)
                            S_.act(sqt, Po, AF.Square)
                            S_.reduce("dve", st[:, 4:8], sqt, ALU.add)
                            S_.ts("dve", st[:, 0:4], st[:, 0:4], 1.0 / 128, None, ALU.mult)
                            S_.tt("dve", st[:, 8:12], st[:, 0:4], st[:, 0:4], ALU.mult)
                            S_.stt("dve", st[:, 4:8], st[:, 4:8], 1.0 / 128, st[:, 8:12], ALU.mult, ALU.subtract)
                            S_.act(st[:, 4:8], st[:, 4:8], AF.Sqrt, bias=EPS)
                            S_.recip(st[:, 4:8], st[:, 4:8])
                            for h in range(4):
                                S_.ts("dve", on[:, h * 128:(h + 1) * 128], Po[:, h, :], st[:, h:h + 1], st[:, 4 + h:5 + h], ALU.subtract, ALU.mult)

                        def stC(ci):
                            tok2feat(on_[ci % 2], zs, ci * 128)

                        stA(0)
                        for ci in range(nchk):
                            if ci + 1 < nchk:
                                stA(ci + 1)
                            stB(ci)
                            if ci >= 1:
                                stC(ci - 1)
                        stC(nchk - 1)
                    stage(20 + 100 * l)
                    ACC = AR.alloc("ACC", (128, 8, n), F32); accbox['a'] = ACC
                    MB16 = AR.alloc("MB16", (128, 8, n), BF16); accbox['m'] = MB16
                    merge(0)

                    stage(21 + 100 * l)
                    with AR.scope(), nscope(f"b3b_L{l}_{g0}"):
                        Wz = wload(ws_gran(l, G_SZ), (8, 512)); Wb_ = wload(ws_gran(l, G_SB), (8, 512))
                        cd = wload(dview(CD_d[l], CD_d[l].ap()[4].rearrange("p (k q) -> p k q", q=128), ((4, 5),)), (31, 128))
                        U = AR.alloc("U", (128, 4, n + 2), BF16)
                        p0 = pidx(g0) - 1
                        S_.dma("sp", U, dview(USC_d, USC_d.ap().rearrange("(c p) t -> p c t", p=128)[:, :, p0:p0 + n + 2], ((0, BW), (p0, p0 + n + 2))))
                        bs = AR.alloc("bs", (128, 4, n), F32); zz = AR.alloc("zz", (128, 4, n), BF16)
                        for ch in range(4):
                            Pz = proj(Wz, ch * 128, blk, n)
                            S_.act(zz[:, ch, :], Pz[:, 0:n], AF.Silu)
                        for ch in range(4):
                            Pb = proj(Wb_, ch * 128, blk, n)
                            S_.tt("dve", bs[:, ch, :], Pb[:, 0:n], zz[:, ch, :], ALU.mult)
                        for ch in range(4):
                            Pc = ps()
                            for k in range(3):
                                S_.mm(Pc[:, 0:n], cd[:, ch * 3 + k, :], U[:, ch, k:k + n], start=(k == 0), stop=(k == 2))
                            S_.tt("dve", BR[:, ch, :], Pc[:, 0:n], bs[:, ch, :], ALU.mult)
                    merge(1)

                    stage(22 + 100 * l)
                    with AR.scope(), nscope(f"b3c_L{l}_{g0}"):
                        zz = AR.alloc("zz", (128, 4, n), BF16)
                        uf = AR.alloc("uf", (128, 4, n), F32)
                        mean = AR.alloc("mean", (128, n), F32); rs = AR.alloc("rs", (128, n), F32)
                        with AR.scope():
                            U = AR.alloc("U", (128, 4, n + 30), BF16)
                            ub_ = [AR.alloc(f"ub{k}", (128, n), BF16) for k in range(2)]
                            us_ = [AR.alloc(f"us{k}", (128, n), BF16) for k in range(2)]
                            p0 = pidx(g0) - 15
                            S_.dma("sp", U, dview(UCF_d, UCF_d.ap().rearrange("(c p) t -> p c t", p=128)[:, :, p0:p0 + n + 30], ((0, BW), (p0, p0 + n + 30))))
                            Pm = pv(6); Pq = pv(7)
                            for ch in range(4):
                                cd = wload(dview(CD_d[l], CD_d[l].ap()[ch].rearrange("p (k q) -> p k q", q=128), ((ch, ch + 1),)), (31, 128))
                                Pc = ps()
                                for k in range(31):
                                    S_.mm(Pc[:, 0:n], cd[:, k, :], U[:, ch, k:k + n], start=(k == 0), stop=(k == 30))
                                bcol = par("cfb", l, slice(ch, ch + 1))
                                ub = ub_[ch % 2]; us = us_[ch % 2]
                                S_.act(uf[:, ch, :], Pc[:, 0:n], AF.Identity, bias=bcol)
                                S_.act(us, Pc[:, 0:n], AF.Square, bias=bcol)
                                S_.copy("dve", ub, uf[:, ch, :])
                                S_.mm(Pm[:, 0:n], ONESC, ub, start=(ch == 0), stop=(ch == 3))
                                S_.mm(Pq[:, 0:n], ONESC, us, start=(ch == 0), stop=(ch == 3))
                            Wz = wload(ws_gran(l, G_CZ), (8, 512))
                            for ch in range(4):
                                Pz = proj(Wz, ch * 128, blk, n)
                                S_.act(zz[:, ch, :], Pz[:, 0:n], AF.Silu)
                            S_.copy("act", mean, Pm[:, 0:n])
                            S_.act(rs, Pm[:, 0:n], AF.Square)
                            S_.tt("dve", rs, Pq[:, 0:n], rs, ALU.subtract)
                            S_.act(rs, rs, AF.Sqrt, bias=EPS)
                            S_.recip(rs, rs)
                        t1_ = [AR.alloc(f"t1{k}", (128, n), F32) for k in range(2)]
                        for ch in range(4):
                            t1 = t1_[ch % 2]
                            S_.tt("dve", t1, uf[:, ch, :], mean, ALU.subtract)
                            S_.tt("dve", t1, t1, rs, ALU.mult)
                            S_.act(t1, t1, AF.Silu, bias=par("lnb", l, slice(ch, ch + 1)), scale=par("lnw", l, slice(ch, ch + 1)))
                            S_.tt("dve", BR[:, ch, :], t1, zz[:, ch, :], ALU.mult)
                    merge(2)

                    stage(23 + 100 * l)
                    with AR.scope(), nscope(f"b3d_L{l}_{g0}"):
                        Wq = wload(ws_gran(l, G_AQ), (8, 512)); Wqs = wload(ws_gran(l, G_AQS), (8, 512)); Wz = wload(ws_gran(l, G_AZ), (8, 512))
                        cs = AR.alloc("cs", (128, 2, n), F32); t1 = AR.alloc("t1", (128, 4, n), F32); t2 = AR.alloc("t2", (128, n), F32)
                        zs = AR.alloc("zs", (128, 4, n), BF16)
                        S_.dma("sp", cs, dview(rope_d, rope_d.ap().rearrange("a p t -> p a t")[:, :, g0:g0 + n], ((0, 1),)))
                        for m in range(4):
                            Pq_ = proj(Wq, m * 128, blk, n)
                            S_.tt("dve", t1[:, m, :], Pq_[:, 0:n], cs[:, 0, :], ALU.mult)
                        for m in range(4):
                            Pqs = proj(Wqs, m * 128, blk, n)
                            S_.tt("dve", t2, Pqs[:, 0:n], cs[:, 1, :], ALU.mult)
                            S_.tt("dve", QA[0:64, m, 0:n], t1[0:64, m, :], t2[0:64, :], ALU.add)
                            S_.tt("dve", QB[64:128, m, 0:n], t1[64:128, m, :], t2[64:128, :], ALU.add)
                        for m in range(4):
                            Pz = proj(Wz, m * 128, blk, n)
                            S_.act(zs[:, m, :], Pz[:, 0:n], AF.Silu)
                        pt_ = [AR.alloc(f"pt{k}", (128, 4, 128), BF16) for k in range(3)]
                        on_ = [AR.alloc(f"on{k}", (128, 8, 64), BF16) for k in range(2)]
                        dn_ = [AR.alloc(f"dn{k}", (128, 8), F32) for k in range(2)]
                        pti = 0
                        for qi in range(nchk):
                            qc = c0 + qi; qs_ = slice(qi * 128, (qi + 1) * 128)
                            if isc:
                                kbs = [(c, None) for c in range(NCC)]
                            else:
                                kbs = []
                                if qc - 1 >= NCC: kbs.append((qc - 1, MPREV))
                                kbs.append((qc, None))
                                if qc + 1 < NCH: kbs.append((qc + 1, MNEXT))
                                kbs += [(c, None) for c in range(NCC)]
                            on = on_[qi % 2]; dn = dn_[qi % 2]
                            Og = [pv(6, F32, (128, 4, 65)), pv(7, F32, (128, 4, 65))]
                            steps = [(bi, kb, msk, g) for bi, (kb, msk) in enumerate(kbs) for g in range(2)]
                            QQ = (QA, QB)

                            def score(step):
                                bi, kb, msk, g = step
                                Pst = ps(F32, (128, 4, 128))
                                S_.mm(Pst, KR[:, kb * 128:(kb + 1) * 128], QQ[g][:, :, qs_])
                                return Pst
                            pend = score(steps[0])
                            for si, step in enumerate(steps):
                                bi, kb, msk, g = step
                                Pst = pend
                                if si + 1 < len(steps):
                                    pend = score(steps[si + 1])
                                pt = pt_[pti % 3]; pti += 1
                                S_.act(pt, Pst, AF.Exp, scale=ATT_SCALE)
                                if msk is not None:
                                    mb = msk.re(msk.ap.unsqueeze(1).to_broadcast([128, 4, 128]))
                                    S_.tt("dve", pt, pt, mb, ALU.mult)
                                for hl in range(4):
                                    S_.mm(Og[g][:, hl, :], pt[:, hl, :], VA[:, kb, g, :], start=(bi == 0 and hl == 0), stop=(bi == len(kbs) - 1),
                                          signal=(bi == len(kbs) - 1 and hl == 3))
                            for g in range(2):
                                O = Og[g]
                                es_ = ESINK[:, g * 4:(g + 1) * 4]
                                S_.tt("dve", dn[:, g * 4:(g + 1) * 4], O[:, :, 64], es_, ALU.add)
                                S_.recip(dn[:, g * 4:(g + 1) * 4], dn[:, g * 4:(g + 1) * 4])
                                dsl = dn[:, g * 4:(g + 1) * 4]
                                rb = dsl.re(dsl.ap.unsqueeze(2).to_broadcast([128, 4, 64]))
                                S_.tt("dve", on[:, g * 4:(g + 1) * 4, :], O[:, :, 0:64], rb, ALU.mult)
                            onf = on.re(on.ap.rearrange("p h e -> p (h e)"))
                            T = ps(BF16, (128, 4, 128))
                            for k in range(4):
                                S_.transpose(T[:, k, :], onf[:, k * 128:(k + 1) * 128], IDB, signal=(k == 3))
                            S_.tt("dve", BR[:, :, qs_], T, zs[:, :, qs_], ALU.mult)
                    merge(3)

                    stage(24 + 100 * l)
                    with AR.scope(), nscope(f"b3e_L{l}_{g0}"):
                        XN = ACC
                        if last:
                            sq = AR.alloc("sq", (128, 8, n), BF16)
                        Wo_ = [wload(dview(WO_d[l], WO_d[l].ap().rearrange("(kt p) n -> p kt n", p=128)[:, :, half * 512:(half + 1) * 512], ((0, D), (half * 512, half * 512 + 512))), (8, 512)) for half in range(2)]
                        if l == 0:
                            for oc in range(8):
                                S_.dma("sp", XN[:, oc, :], xchunk_src(l, oc, g0, n, isc))
                        for half in range(2):
                            Wo = Wo_[half]
                            for o4 in range(4):
                                oc = half * 4 + o4
                                Y = ps()
                                for kt in range(8):
                                    S_.mm(Y[:, 0:n], Wo[:, kt, o4 * 128:(o4 + 1) * 128], MB16[:, kt, :], start=(kt == 0), stop=(kt == 7))
                                S_.stt("dve", XN[:, oc, :], Y[:, 0:n], Gvec(l, w, oc), XN[:, oc, :], ALU.mult, ALU.add)
                                if not last:
                                    S_.dma(STQ, dview(nxt, nxt.ap()[oc * 128:(oc + 1) * 128, g0:g0 + n], ((oc * 128, oc * 128 + 128), (g0, g0 + n))), XN[:, oc, :])
                        if last:
                            S_.act(sq, XN, AF.Square)
                            P = ps()
                            for j in range(8):
                                S_.mm(P[:, 0:n], ONESD, sq[:, j, :], start=(j == 0), stop=(j == 7))
                            rstd = AR.alloc("rstd", (128, n), F32)
                            S_.act(rstd, P[:, 0:n], AF.Sqrt, bias=EPS)
                            S_.recip(rstd, rstd)
                            for j in range(8):
                                S_.stt("dve", XN[:, j, :], XN[:, j, :], par("fnw", slice(j, j + 1)), rstd, ALU.mult, ALU.mult)
                            t0_ = g0 - L
                            S_.dma("sp", dview(yT_d, yT_d.ap().rearrange("(j p) t -> p j t", p=128)[:, :, t0_:t0_ + n], ((0, D), (t0_, t0_ + n))), XN)
        S_.wait_all("sp", [dmem["yT"]])
        print(f"[build] ops={S_.nops} waits={S_.nwaits} sems={S_.nsem} arena_top={AR.top}")
    return nc


def make_in_maps(inp, S, L, DEPTH, nb):
    C, R = host_consts(S, L)
    maps = []
    for b in range(nb):
        maps.append({
            "xT": np.ascontiguousarray(inp["x"][b].T), "cT": np.ascontiguousarray(inp["ctx"][b].T),
            "par": host_params(inp, b, DEPTH), "con": C, "rope": R,
            "w_mod": inp["w_mod"][:DEPTH], "w_in": inp["w_in"][:DEPTH], "w_branch": inp["w_branch"][:DEPTH], "w_out": inp["w_out"][:DEPTH],
        })
    return maps


def kernel(**inputs):
    from concourse.bass_utils import run_bass_kernel_spmd
    inp = {k: np.asarray(v) for k, v in inputs.items()}
    B, S, _ = inp["x"].shape; L = inp["ctx"].shape[1]; DEPTH = inp["w_in"].shape[0]
    nc = build_program(S, L, DEPTH)
    maps = make_in_maps(inp, S, L, DEPTH, B)
    res = run_bass_kernel_spmd(nc, maps, core_ids=list(range(B)))
    out = np.stack([np.ascontiguousarray(r["yT"].T) for r in res.results], 0)
    return out.astype(np.float32)
```

```python
import numpy as np, contextlib
import concourse.bass as bass
import concourse.mybir as mybir

F32 = mybir.dt.float32
BF16 = mybir.dt.bfloat16
AF = mybir.ActivationFunctionType
ALU = mybir.AluOpType
AX = mybir.AxisListType
_ISZ = {F32: 4, BF16: 2}


def _norm_idx(idx, shape):
    if not isinstance(idx, tuple):
        idx = (idx,)
    idx = list(idx) + [slice(None)] * (len(shape) - len(idx))
    out = []
    for i, n in zip(idx, shape):
        if isinstance(i, int):
            out.append((i, i + 1, True))
        else:
            lo = 0 if i.start is None else i.start
            hi = n if i.stop is None else i.stop
            assert i.step in (None, 1) and 0 <= lo < hi <= n, (i, n)
            out.append((lo, hi, False))
    return out


class Mem:
    def __init__(self, name, birth=None, excl=False):
        self.name = name
        self.excl = excl
        self.recs = []
        self.birth = dict(birth or {})

    def summary(self):
        d = dict(self.birth)
        for _, w, r in self.recs:
            _merge(d, w)
            _merge(d, r)
        return d


def _merge(d, s):
    for k, v in s.items():
        if d.get(k, 0) < v:
            d[k] = v


def _ovl(a, b):
    for (al, ah), (bl, bh) in zip(a, b):
        if al >= bh or bl >= ah:
            return False
    return True


def _covers(a, b):
    for (al, ah), (bl, bh) in zip(a, b):
        if al > bl or ah < bh:
            return False
    return True


class V:
    __slots__ = ("ap", "mem", "box", "shape", "base")

    def __init__(self, ap, mem, box, shape, base=None):
        self.ap = ap
        self.mem = mem
        self.box = box
        self.shape = shape
        self.base = base

    def __getitem__(self, idx):
        if self.base is None:
            return V(self.ap[idx], self.mem, self.box, None, None)
        n = _norm_idx(idx, self.shape)
        box = list(self.box)
        newbase = []
        newshape = []
        for (lo, hi, drop), (md, off) in zip(n, self.base):
            box[md] = (off + lo, off + hi)
            if not drop:
                newbase.append((md, off + lo))
                newshape.append(hi - lo)
        return V(self.ap[idx], self.mem, tuple(box), tuple(newshape), newbase)

    def re(self, ap):
        return V(ap, self.mem, self.box, tuple(ap.shape), None)


def mkview(ap, mem, shape):
    return V(ap, mem, tuple((0, s) for s in shape), tuple(shape), [(i, 0) for i in range(len(shape))])


class Sched:
    SAME_ENGINE_SYNC = True
    EPOCH = 30000
    NQ = 8

    def __init__(self, nc, es):
        self.nc = nc
        self.es = es
        self.E = {"pe": nc.tensor, "dve": nc.vector, "act": nc.scalar, "pool": nc.gpsimd, "sp": nc.sync}
        self.semobj = {}
        self.nsem = 0
        self.cur = {}
        self.cnt = {}
        self.mysems = {e: set() for e in self.E}
        for e in self.E:
            self._new_epoch(e)
        self.waited = {e: {} for e in self.E}
        self.dq = {}
        for q in ("sp", "pool", "act"):
            sems = [self._sem(f"dq_{q}{i}") for i in range(self.NQ)]
            self.dq[q] = {"sems": sems, "i": 0}
        self.nops = 0
        self.nwaits = 0

    def _sem(self, name):
        s = self.es.enter_context(self.nc.semaphore(name))
        sid = self.nsem
        self.nsem += 1
        self.semobj[sid] = s
        return sid

    def _new_epoch(self, e):
        sid = self._sem(f"e_{e}{self.nsem}")
        self.cur[e] = sid
        self.cnt[e] = 0
        self.mysems[e].add(sid)

    def _deps(self, reads, writes):
        d = {}
        for v in reads:
            m = v.mem
            _merge(d, m.birth)
            for box, w, r in m.recs:
                if _ovl(box, v.box):
                    _merge(d, w)
                    if m.excl:
                        _merge(d, r)
        for v in writes:
            m = v.mem
            _merge(d, m.birth)
            for box, w, r in m.recs:
                if _ovl(box, v.box):
                    _merge(d, w)
                    _merge(d, r)
        return d

    def _record(self, reads, writes, tok):
        sid, val = tok
        for v in reads:
            for rec in v.mem.recs:
                if _ovl(rec[0], v.box):
                    if rec[2].get(sid, 0) < val:
                        rec[2][sid] = val
        for v in writes:
            m = v.mem
            keep = []
            carry_r = {}
            for rec in m.recs:
                if _covers(v.box, rec[0]):
                    continue
                keep.append(rec)
            keep.append([v.box, {sid: val}, {}])
            m.recs = keep

    def _emit_waits(self, eng, deps):
        w = self.waited[eng]
        E = self.E[eng]
        for sid, val in deps.items():
            if sid in self.mysems[eng]:
                if eng == "pe" or eng == "sp" or not self.SAME_ENGINE_SYNC:
                    continue
            if w.get(sid, 0) < val:
                E.wait_ge(self.semobj[sid], val)
                w[sid] = val
                self.nwaits += 1

    def issue(self, eng, fn, reads, writes, signal=True):
        deps = self._deps(reads, writes)
        self._emit_waits(eng, deps)
        inst = fn()
        self.nops += 1
        sid = self.cur[eng]
        if signal:
            self.cnt[eng] += 1
            inst.then_inc(self.semobj[sid], 1)
            tok = (sid, self.cnt[eng])
        else:
            tok = (sid, self.cnt[eng] + 1)
        self._record(reads, writes, tok)
        if signal and self.cnt[eng] >= self.EPOCH:
            self._new_epoch(eng)
        return tok

    def dma(self, q, out, in_, **kw):
        deps = self._deps([in_], [out])
        D = self.dq[q]
        i = D["i"]
        D["i"] += 1
        sid = D["sems"][i % self.NQ]
        rnd = i // self.NQ
        if rnd > 0:
            deps_prev = {sid: 16 * rnd}
            _merge(deps, deps_prev)
        self._emit_waits(q, deps)
        inst = self.E[q].dma_start(out=out.ap, in_=in_.ap, **kw)
        inst.then_inc(self.semobj[sid], 16)
        tok = (sid, 16 * (rnd + 1))
        self._record([in_], [out], tok)
        self.nops += 1
        return tok

    def wait_all(self, eng, mems):
        d = {}
        for m in mems:
            _merge(d, m.summary())
        self._emit_waits(eng, d)

    def mm(self, out, lhsT, rhs, start=True, stop=True, signal=None, **kw):
        if signal is None:
            signal = stop
        return self.issue("pe", lambda: self.nc.tensor.matmul(out.ap, lhsT=lhsT.ap, rhs=rhs.ap, start=start, stop=stop, **kw),
                          [lhsT, rhs], [out], signal=signal)

    def transpose(self, out, in_, ident, signal=True):
        return self.issue("pe", lambda: self.nc.tensor.transpose(out=out.ap, in_=in_.ap, identity=ident.ap),
                          [in_, ident], [out], signal=signal)

    def act(self, out, in_, func, bias=None, scale=None, eng="act"):
        reads = [in_]
        kw = {}
        if bias is not None:
            if isinstance(bias, V):
                reads.append(bias); kw["bias"] = bias.ap
            else:
                kw["bias"] = float(bias)
        if scale is not None:
            if isinstance(scale, V):
                reads.append(scale); kw["scale"] = scale.ap
            else:
                kw["scale"] = float(scale)
        return self.issue("act", lambda: self.nc.scalar.activation(out=out.ap, in_=in_.ap, func=func, **kw), reads, [out])

    def tt(self, eng, out, in0, in1, op):
        return self.issue(eng, lambda: self.E[eng].tensor_tensor(out=out.ap, in0=in0.ap, in1=in1.ap, op=op), [in0, in1], [out])

    def ts(self, eng, out, in0, s1, s2, op0, op1=None):
        reads = [in0]
        a1 = s1.ap if isinstance(s1, V) else s1
        a2 = s2.ap if isinstance(s2, V) else s2
        if isinstance(s1, V): reads.append(s1)
        if isinstance(s2, V): reads.append(s2)
        if op1 is None:
            f = lambda: self.E[eng].tensor_scalar(out=out.ap, in0=in0.ap, scalar1=a1, scalar2=None, op0=op0)
        else:
            f = lambda: self.E[eng].tensor_scalar(out=out.ap, in0=in0.ap, scalar1=a1, scalar2=a2, op0=op0, op1=op1)
        return self.issue(eng, f, reads, [out])

    def stt(self, eng, out, in0, scalar, in1, op0, op1):
        reads = [in0, in1]
        a = scalar.ap if isinstance(scalar, V) else scalar
        if isinstance(scalar, V): reads.append(scalar)
        return self.issue(eng, lambda: self.E[eng].scalar_tensor_tensor(out=out.ap, in0=in0.ap, scalar=a, in1=in1.ap, op0=op0, op1=op1),
                          reads, [out])

    def copy(self, eng, out, in_):
        if eng == "act":
            return self.issue("act", lambda: self.nc.scalar.copy(out=out.ap, in_=in_.ap), [in_], [out])
        return self.issue(eng, lambda: self.E[eng].tensor_copy(out=out.ap, in_=in_.ap), [in_], [out])

    def memset(self, eng, out, val):
        return self.issue(eng, lambda: self.E[eng].memset(out.ap, val), [], [out])

    def reduce(self, eng, out, in_, op, axis=AX.X):
        return self.issue(eng, lambda: self.E[eng].tensor_reduce(out=out.ap, in_=in_.ap, axis=axis, op=op), [in_], [out])

    def recip(self, out, in_):
        return self.issue("dve", lambda: self.nc.vector.reciprocal(out=out.ap, in_=in_.ap), [in_], [out])


class Arena:
    def __init__(self, sched, ap_f32, nwords):
        self.s = sched
        self.ap = ap_f32
        self.n = nwords
        self.top = 0
        self.dead = []
        self.live = []

    def alloc(self, name, shape, dtype, p=128):
        isz = _ISZ[dtype]
        free = int(np.prod(shape[1:]))
        nbytes = free * isz
        nw = (nbytes + 3) // 4
        lo, hi = self.top, self.top + nw
        assert hi <= self.n, f"arena overflow allocating {name}: {hi} > {self.n}"
        self.top = hi
        birth = {}
        nd = []
        for (dl, dh, toks) in self.dead:
            if dl < hi and lo < dh:
                _merge(birth, toks)
                if lo <= dl and dh <= hi:
                    continue
            nd.append((dl, dh, toks))
        self.dead = nd
        mem = Mem(name, birth)
        ap = self.ap[0:shape[0], lo:hi]
        if dtype != F32:
            ap = ap.bitcast(dtype)
        ap = ap[:, 0:free]
        if len(shape) > 2:
            names = " ".join(f"d{i}" for i in range(1, len(shape)))
            kw = {f"d{i}": shape[i] for i in range(2, len(shape))}
            ap = ap.rearrange(f"p ({names}) -> p {names}", **kw)
        self.live.append((lo, hi, mem))
        return mkview(ap, mem, shape)

    def mark(self):
        return (self.top, len(self.live))

    def release(self, mark):
        top, nl = mark
        while len(self.live) > nl:
            lo, hi, mem = self.live.pop()
            self.dead.append((lo, hi, mem.summary()))
        self.top = top

    @contextlib.contextmanager
    def scope(self):
        m = self.mark()
        yield
        self.release(m)


D = 1024; BW = 512; KS_RET = 128 ** -0.5; ATT_SCALE = 64 ** -0.5; EPS = 1e-6
IN_OFF = {}
_o = 0
for _n, _s in zip(('ret_q','ret_k','ret_v','ret_z','sc_b','sc_c','sc_x','sc_z','cf_a','cf_g','cf_z','att_q','att_k','att_v','att_z','gates'),
                  (512,512,512,512,512,512,512,512,512,512,512,512,128,128,512,4096)):
    IN_OFF[_n] = _o; _o += _s
IN_TOTAL = _o
NGRAN = 23
G_RQ, G_RK, G_RV, G_RZ, G_SB, G_SC, G_SX, G_SZ, G_CA, G_CG, G_CZ, G_AQ, G_AQS, G_AKV, G_AZ, G_GATE = 0,1,2,3,4,5,6,7,8,9,10,11,12,13,14,15


def par_layout(DEPTH):
    lay = {}; off = 0
    for name, shp in (("b_mod", (DEPTH, 24)), ("norm_w", (DEPTH, 8)), ("fnw", (8,)), ("gn_w", (DEPTH, 4)), ("scw", (DEPTH, 4, 3)),
                      ("cfw", (DEPTH, 4, 31)), ("cfb", (DEPTH, 4)), ("lnw", (DEPTH, 4)), ("lnb", (DEPTH, 4)),
                      ("rdec", (DEPTH, 8)), ("sink", (DEPTH, 8)), ("cvec", (8, 2))):
        n = int(np.prod(shp)); lay[name] = (off, shp); off += n
    return lay, off

CON_NAMES = ("ident", "posdF", "posdB", "MF", "MB", "rampF1", "rampB", "c128", "onesf", "mprev", "mnext")

def con_layout():
    lay = {}; off = 0
    for nm in CON_NAMES:
        lay[nm] = off; off += 128
    lay["rp127"] = off; off += 1
    lay["rp"] = off; off += 1
    return lay, off


def host_consts(S, L):
    lay, n = con_layout()
    C = np.zeros((128, n), np.float32)
    j = np.arange(128)[:, None].astype(np.float32); i = np.arange(128)[None, :].astype(np.float32)
    def put(nm, a): C[:, lay[nm]:lay[nm] + 128] = a
    put("ident", np.eye(128)); put("posdF", np.maximum(i - j, 0)); put("posdB", np.maximum(j - i, 0))
    put("MF", (i >= j)); put("MB", (j > i)); put("rampF1", np.broadcast_to(i + 1, (128, 128))); put("rampB", np.broadcast_to(128 - i, (128, 128)))
    put("c128", np.full((128, 128), 128.0)); put("onesf", np.ones((128, 128)))
    put("mprev", (j >= i)); put("mnext", (j <= i))
    C[:, lay["rp127"]] = 127 - np.arange(128); C[:, lay["rp"]] = np.arange(128)
    NT = L + S
    t = np.arange(S); row = (t // 64).astype(np.float32); col = (t % 64).astype(np.float32)
    inv = (10000.0 ** (-np.arange(16, dtype=np.float32) / 16)).astype(np.float32)
    ang = np.concatenate([row[:, None] * inv[None], col[:, None] * inv[None]], -1).astype(np.float32)
    cos = np.cos(ang).astype(np.float32); sin = np.sin(ang).astype(np.float32)
    R = np.zeros((2, 128, NT), np.float32); R[0, :, :L] = 1.0
    for p in range(128):
        d = p % 64
        R[0, p, L:] = cos[:, d % 32]
        R[1, p, L:] = (-sin[:, d % 32]) if d < 32 else sin[:, d % 32]
    return C, R


def host_params(inp, b, DEPTH):
    lay, n = par_layout(DEPTH)
    P = np.zeros((128, n), np.float32)
    def put(nm, arr):
        off, shp = lay[nm]; P[:, off:off + int(np.prod(shp))] = arr.reshape(128, -1)
    fm = lambda a, k: np.ascontiguousarray(a.reshape(a.shape[:-1] + (k, 128)))
    put("b_mod", np.moveaxis(fm(inp["b_mod"][:DEPTH], 24), -1, 0))
    put("norm_w", np.moveaxis(fm(inp["norm_w"][:DEPTH], 8), -1, 0))
    put("fnw", np.moveaxis(fm(inp["final_norm_w"], 8), -1, 0))
    put("gn_w", np.moveaxis(fm(inp["ret_gn_w"][:DEPTH], 4), -1, 0))
    put("scw", np.transpose(inp["sc_conv_w"][:DEPTH].reshape(DEPTH, 3, 4, 128), (3, 0, 2, 1)))
    put("cfw", np.transpose(inp["cf_conv_w"][:DEPTH].reshape(DEPTH, 31, 4, 128), (3, 0, 2, 1)))
    put("cfb", np.moveaxis(fm(inp["cf_conv_b"][:DEPTH], 4), -1, 0))
    put("lnw", np.moveaxis(fm(inp["cf_ln_w"][:DEPTH], 4), -1, 0))
    put("lnb", np.moveaxis(fm(inp["cf_ln_b"][:DEPTH], 4), -1, 0))
    put("rdec", np.broadcast_to(inp["ret_decay"][:DEPTH].reshape(1, DEPTH, 8), (128, DEPTH, 8)))
    put("sink", np.broadcast_to(inp["att_sink"][:DEPTH].reshape(1, DEPTH, 8), (128, DEPTH, 8)))
    cv = np.stack([inp["c"][b].reshape(8, 128), inp["c_ctx"].reshape(8, 128)], -1)
    put("cvec", np.transpose(cv, (1, 0, 2)))
    return P


class StopBuild(Exception):
    pass


STQ = "act"


def build_program(S, L, DEPTH, TB=512, STOP=None, DEBUG=False):
    def stage(k):
        if STOP is not None and k == STOP:
            raise StopBuild()
    NT = L + S; NCH = NT // 128; NCC = L // 128
    NTP = NT + 64
    pidx = lambda g: g + 16 if g < L else g + 48
    blocks = [(0, L, True)] + [(L + i * TB, TB, False) for i in range(S // TB)]
    playout, NPAR = par_layout(DEPTH); clayout, NCON = con_layout()
    nc = bass.Bass("TRN2", target_bir_lowering=False)
    dt = nc.dram_tensor
    xT_d = dt("xT", [D, S], F32, kind="ExternalInput"); cT_d = dt("cT", [D, L], F32, kind="ExternalInput")
    par_d = dt("par", [128, NPAR], F32, kind="ExternalInput"); con_d = dt("con", [128, NCON], F32, kind="ExternalInput")
    rope_d = dt("rope", [2, 128, NT], F32, kind="ExternalInput")
    wmod_d = dt("w_mod", [DEPTH, D, 3 * D], F32, kind="ExternalInput")
    win_d = dt("w_in", [DEPTH, D, IN_TOTAL], F32, kind="ExternalInput")
    wbr_d = dt("w_branch", [DEPTH, 4, BW, D], F32, kind="ExternalInput")
    wout_d = dt("w_out", [DEPTH, D, D], F32, kind="ExternalInput")
    yT_d = dt("yT", [D, S], F32, kind="ExternalOutput")
    XS_d = [dt(f"XS{i}", [D, NT], F32, kind="Internal") for i in range(2)]
    WS_d = [dt(f"WS{l}", [D, NGRAN * 512], BF16, kind="Internal") for l in range(DEPTH)]
    WB_d = [dt(f"WB{l}", [4, BW, D], BF16, kind="Internal") for l in range(DEPTH)]
    WO_d = [dt(f"WO{l}", [D, D], BF16, kind="Internal") for l in range(DEPTH)]
    CD_d = [dt(f"CD{l}", [5, 128, 31 * 128], BF16, kind="Internal") for l in range(DEPTH)]
    dk = "ExternalOutput" if DEBUG else "Internal"
    SF_d = dt("SFs", [NCH, 128, 512], BF16, kind=dk); SB_d = dt("SBs", [NCH, 128, 512], BF16, kind=dk)
    USC_d = dt("USC", [BW, NTP], BF16, kind=dk); UCF_d = dt("UCF", [BW, NTP], BF16, kind=dk)
    DBG_d = dt("DBG", [4, BW, NT], BF16, kind=dk) if DEBUG else None

    es = contextlib.ExitStack()
    with es:
        es.push(lambda et, ev, tb: et is not None and issubclass(et, StopBuild))
        es.enter_context(nc.allow_low_precision("bf16 matmul operands, fp32 accumulation"))
        es.enter_context(nc.allow_non_contiguous_dma("small strided weight gathers"))
        S_ = Sched(nc, es)
        NW = 52800
        ar_t = es.enter_context(nc.sbuf_tensor("arena", [128, NW], F32))
        AR = Arena(S_, ar_t[:, :], NW)
        psb = [es.enter_context(nc.psum_tensor(f"psb{i}", [128, 512], F32)) for i in range(8)]
        psm = [Mem(f"ps{i}", excl=True) for i in range(8)]
        pstate = {"i": 0}

        def pv(bank, dtype=F32, shape=(128, 512)):
            n = int(np.prod(shape[1:]))
            if dtype == F32:
                ap = psb[bank][:, 0:n]
            else:
                ap = psb[bank][:, 0:(n + 1) // 2].bitcast(BF16)[:, 0:n]
            if len(shape) > 2:
                names = " ".join(f"d{i}" for i in range(1, len(shape)))
                kw = {f"d{i}": shape[i] for i in range(2, len(shape))}
                ap = ap.rearrange(f"p ({names}) -> p {names}", **kw)
            return V(ap, psm[bank], ((0, 1),), None, None)

        def ps(dtype=F32, shape=(128, 512)):
            b = pstate["i"] % 6; pstate["i"] += 1
            return pv(b, dtype, shape)

        def dview(t, ap, box):
            return V(ap, dmem[t.name], box, None, None)
        dmem = {}
        for t in [xT_d, cT_d, par_d, con_d, rope_d, wmod_d, win_d, wbr_d, wout_d, yT_d, SF_d, SB_d, USC_d, UCF_d] + XS_d + WS_d + WB_d + WO_d + CD_d:
            dmem[t.name] = Mem(t.name)

        PAR = AR.alloc("PAR", (128, NPAR), F32); CON = AR.alloc("CON", (128, NCON), F32)
        S_.dma("sp", PAR, dview(par_d, par_d.ap(), ((0, 128), (0, NPAR))))
        S_.dma("sp", CON, dview(con_d, con_d.ap(), ((0, 128), (0, NCON))))
        def par(nm, *idx):
            off, shp = playout[nm]; n = int(np.prod(shp))
            v = PAR[:, off:off + n]
            if len(shp) > 1:
                names = " ".join(f"d{i}" for i in range(len(shp)))
                kw = {f"d{i}": shp[i] for i in range(1, len(shp))}
                v = v.re(v.ap.rearrange(f"p ({names}) -> p {names}", **kw))
            return v[(slice(None),) + idx] if idx else v
        def con(nm, w=128):
            o = clayout[nm]; return CON[:, o:o + w]
        HX = AR.alloc("HX", (128, 8, NT), BF16)
        KR = AR.alloc("KR", (128, NT), BF16)
        VA = AR.alloc("VA", (128, NCH, 2, 65), BF16)
        QA = AR.alloc("QA", (128, 4, TB), BF16); QB = AR.alloc("QB", (128, 4, TB), BF16)
        NWR = 4
        WR = [AR.alloc(f"WR{i}", (128, 4096), BF16) for i in range(NWR)]
        wst = {"i": 0}
        IDB = AR.alloc("IDB", (128, 128), BF16); ONESD = AR.alloc("ONESD", (128, 128), BF16); ONESC = AR.alloc("ONESC", (128, 128), BF16)
        MPREV = AR.alloc("MPREV", (128, 128), BF16); MNEXT = AR.alloc("MNEXT", (128, 128), BF16)
        MOD = AR.alloc("MOD", (128, DEPTH, 24, 2), F32)
        AXC = AR.alloc("AXC", (128, DEPTH, 2, 8), F32)
        LG = AR.alloc("LG", (128, 8), F32); ESINK = AR.alloc("ESINK", (128, 8), F32)
        DT = AR.alloc("DT", (128, 4, 128), F32); QWF = AR.alloc("QWF", (128, 4, 128), F32); QWB = AR.alloc("QWB", (128, 4, 128), F32)
        KWF = AR.alloc("KWF", (128, 512), F32); KWB = AR.alloc("KWB", (128, 512), F32)
        GF = AR.alloc("GF", (128, 512), F32); GB = AR.alloc("GB", (128, 512), F32)
        SMALL = AR.alloc("SMALL", (128, 64), F32)
        S_.copy("dve", IDB, con("ident")); S_.memset("dve", ONESD, 1.0 / 1024); S_.memset("dve", ONESC, 1.0 / 512)
        S_.copy("dve", MPREV, con("mprev")); S_.copy("dve", MNEXT, con("mnext"))
        S_.memset("pool", VA, 1.0); S_.memset("pool", QA, 0.0); S_.memset("pool", QB, 0.0)

        def wload(src_view, shape3):
            slot = WR[wst["i"] % NWR]; wst["i"] += 1
            a, b = shape3
            v = slot[:, 0:a * b]
            v = v.re(v.ap.rearrange("p (a b) -> p a b", b=b))
            S_.dma("sp", v, src_view)
            return v

        def ws_gran(l, g):
            ap = WS_d[l].ap().rearrange("(kt p) n -> p kt n", p=128)[:, :, g * 512:(g + 1) * 512]
            return dview(WS_d[l], ap, ((0, D), (g * 512, (g + 1) * 512)))

        with AR.scope():
            Z = AR.alloc("Z", (128, 4, 32), BF16)
            S_.memset("dve", Z, 0.0)
            for U_d in (USC_d, UCF_d):
                for (lo, w) in ((0, 16), (16 + L, 32), (NTP - 16, 16)):
                    ap = U_d.ap().rearrange("(c p) t -> p c t", p=128)[:, :, lo:lo + w]
                    S_.dma("sp", dview(U_d, ap, ((0, BW), (lo, lo + w))), Z[:, :, 0:w])
        def xblock_src(l, g0, n, isc):
            if l == 0:
                t, o = (cT_d, g0) if isc else (xT_d, g0 - L)
            else:
                t, o = XS_d[l % 2], g0
            return dview(t, t.ap().rearrange("(j p) t -> p j t", p=128)[:, :, o:o + n], ((0, D), (o, o + n)))

        def xchunk_src(l, oc, g0, n, isc):
            if l == 0:
                t, o = (cT_d, g0) if isc else (xT_d, g0 - L)
            else:
                t, o = XS_d[l % 2], g0
            return dview(t, t.ap()[oc * 128:(oc + 1) * 128, o:o + n], ((oc * 128, oc * 128 + 128), (o, o + n)))

        def cast(dst_t, dst_ap, dbox, src_t, src_ap):
            S_.dma("pool", dview(dst_t, dst_ap, dbox), dview(src_t, src_ap, ((0, 1),)))

        def cast_layer(l):
            W = win_d.ap()[l]; Wd = WS_d[l].ap()
            def seg(gc, c0, n, src0):
                d0 = gc
                cast(WS_d[l], Wd[:, d0:d0 + n], ((0, D), (d0, d0 + n)), win_d, W[:, src0:src0 + n])
            names = ['ret_q','ret_k','ret_v','ret_z','sc_b','sc_c','sc_x','sc_z','cf_a','cf_g','cf_z']
            for gi in (1, 2, 5, 6, 8, 9):
                seg(gi * 512, 0, 512, gi * 512)
            aq = IN_OFF['att_q']
            for m in range(4):
                for hh, h in enumerate((m, 4 + m)):
                    seg(G_AQ * 512 + m * 128 + hh * 64, 0, 64, aq + h * 64)
                    seg(G_AQS * 512 + m * 128 + hh * 64, 0, 32, aq + h * 64 + 32)
                    seg(G_AQS * 512 + m * 128 + hh * 64 + 32, 0, 32, aq + h * 64)
            ak = IN_OFF['att_k']; av = IN_OFF['att_v']
            seg(G_AKV * 512, 0, 128, ak)
            for g in range(2):
                seg(G_AKV * 512 + 128 + g * 64, 0, 32, ak + g * 64 + 32)
                seg(G_AKV * 512 + 128 + g * 64 + 32, 0, 32, ak + g * 64)
            seg(G_AKV * 512 + 256, 0, 128, av)
            seg(G_AKV * 512 + 384, 0, 128, av)
            for gi in (0, 3, 4, 7, 10):
                seg(gi * 512, 0, 512, gi * 512)
            seg(G_AZ * 512, 0, 512, IN_OFF['att_z'])
            seg(G_GATE * 512, 0, 4096, IN_OFF['gates'])
            cast(WB_d[l], WB_d[l].ap(), ((0, 4), (0, BW), (0, D)), wbr_d, wbr_d.ap()[l])
            cast(WO_d[l], WO_d[l].ap(), ((0, D), (0, D)), wout_d, wout_d.ap()[l])

        stage(0)

        SC = AR.alloc("SC", (128, 8, 2), BF16)
        S_.act(SC, par("cvec"), AF.Silu)

        def do_mod(l):
            for jc in range(6):
                slot = WR[wst["i"] % NWR]; wst["i"] += 1
                wv = slot.re(slot.ap.rearrange("p (a b) -> p a b", b=512))
                src = wmod_d.ap()[l].rearrange("(kt p) n -> p kt n", p=128)[:, :, jc * 512:(jc + 1) * 512]
                S_.dma("pool", wv, dview(wmod_d, src, ((0, 1),)))
                P = ps()
                for j4 in range(4):
                    for kt in range(8):
                        S_.mm(P[:, j4 * 2:j4 * 2 + 2], wv[:, kt, j4 * 128:(j4 + 1) * 128], SC[:, kt, :], start=(kt == 0), stop=(kt == 7))
                for j4 in range(4):
                    j = jc * 4 + j4
                    S_.ts("dve", MOD[:, l, j, :], P[:, j4 * 2:j4 * 2 + 2], par("b_mod", l, slice(j, j + 1)), None, ALU.add)
            for w in range(2):
                S_.stt("dve", AXC[:, l, w, :], MOD[:, l, 8:16, w], 1.0, par("norm_w", l), ALU.add, ALU.mult)

        def do_diag(l):
            saved = AR.top
            AR.top = NW - 1984
            m = AR.mark()
            stg_ = [AR.alloc("stg0", (128, 31, 128), BF16)]
            for gi in range(5):
                stg = stg_[0]
                nt = 31 if gi < 4 else 12
                for k in range(nt):
                    wcol = par("cfw", l, gi, slice(k, k + 1)) if gi < 4 else par("scw", l, k // 3, slice(k % 3, k % 3 + 1))
                    S_.ts("pool", stg[:, k, :], con("ident"), wcol, None, ALU.mult)
                ap = CD_d[l].ap()[gi].rearrange("p (k q) -> p k q", q=128)[:, 0:nt, :]
                S_.dma("pool", dview(CD_d[l], ap, ((gi, gi + 1),)), stg[:, 0:nt, :])
            AR.release(m)
            AR.top = saved

        do_mod(0)
        stage(1)
        for l in range(DEPTH):
            cast_layer(l)
            do_diag(l)
        stage(3)

        def nscope(name):
            return nc.named_scope(name) if DEBUG else contextlib.nullcontext()

        def Avec(l, w, j): return AXC[:, l, w, j:j + 1]
        def Bvec(l, w, j): return MOD[:, l, j, w:w + 1]
        def Gvec(l, w, j): return MOD[:, l, 16 + j, w:w + 1]

        def proj(wv, c0, blk, n, out=None):
            P = out if out is not None else ps()
            for kt in range(8):
                S_.mm(P[:, 0:n], wv[:, kt, c0:c0 + 128], HX[:, kt, blk], start=(kt == 0), stop=(kt == 7))
            return P

        for l in range(DEPTH):
            last = (l == DEPTH - 1)
            cur, nxt = XS_d[l % 2], XS_d[(l + 1) % 2]
            with AR.scope():
                t0 = AR.alloc("t0", (128, 8), F32); e1 = AR.alloc("e1", (128, 128), F32); e2 = AR.alloc("e2", (128, 128), F32)
                kw1 = AR.alloc("kw1", (128, 8), F32)
                S_.act(t0, par("rdec", l), AF.Exp, scale=-1.0)
                S_.act(t0, t0, AF.Ln, bias=1.0)
                S_.ts("dve", LG, t0, -1.0, None, ALU.mult)
                S_.act(ESINK, par("sink", l), AF.Exp)
                for h in range(4):
                    S_.act(e1, con("posdF"), AF.Exp, scale=LG[:, h:h + 1])
                    S_.tt("dve", e1, e1, con("MF"), ALU.mult)
                    S_.act(e2, con("posdB"), AF.Exp, scale=LG[:, 4 + h:5 + h])
                    S_.tt("dve", e2, e2, con("MB"), ALU.mult)
                    S_.tt("dve", e1, e1, e2, ALU.add)
                    S_.ts("dve", DT[:, h, :], e1, KS_RET, None, ALU.mult)
                    S_.act(QWF[:, h, :], con("rampF1"), AF.Exp, scale=LG[:, h:h + 1])
                    S_.act(QWB[:, h, :], con("rampB"), AF.Exp, scale=LG[:, 4 + h:5 + h])
                    S_.act(kw1[:, h:h + 1], con("rp127", 1), AF.Exp, scale=LG[:, h:h + 1])
                    S_.act(kw1[:, 4 + h:5 + h], con("rp", 1), AF.Exp, scale=LG[:, 4 + h:5 + h])
                    S_.ts("dve", KWF[:, h * 128:(h + 1) * 128], con("onesf"), kw1[:, h:h + 1], KS_RET, ALU.mult, ALU.mult)
                    S_.ts("dve", KWB[:, h * 128:(h + 1) * 128], con("onesf"), kw1[:, 4 + h:5 + h], KS_RET, ALU.mult, ALU.mult)
                    S_.act(GF[:, h * 128:(h + 1) * 128], con("c128"), AF.Exp, scale=LG[:, h:h + 1])
                    S_.act(GB[:, h * 128:(h + 1) * 128], con("c128"), AF.Exp, scale=LG[:, 4 + h:5 + h])
            stage(10 + 100 * l)
            with AR.scope():
                sets = []
                sqw = AR.alloc("sqw", (128, 8, TB), BF16)
                for k in range(2):
                    sets.append(dict(xb=AR.alloc(f"xb{k}", (128, 8, TB), F32), rstd=AR.alloc(f"rstd{k}", (128, TB), F32),
                                     tmp=[AR.alloc(f"tmp{k}{q}", (128, TB), F32) for q in range(1)]))
                for bi_, (g0, n, isc) in enumerate(blocks):
                    w = 1 if isc else 0
                    B_ = sets[bi_ % 2]
                    with nscope(f"p1_L{l}_{g0}"):
                        xb = B_["xb"][:, :, 0:n]; rstd = B_["rstd"][:, 0:n]; sq = sqw[:, :, 0:n]
                        S_.dma("sp", xb, xblock_src(l, g0, n, isc))
                        S_.act(sq, xb, AF.Square)
                        P = ps()
                        for j in range(8):
                            S_.mm(P[:, 0:n], ONESD, sq[:, j, :], start=(j == 0), stop=(j == 7))
                        S_.act(rstd, P[:, 0:n], AF.Sqrt, bias=EPS)
                        S_.recip(rstd, rstd)
                        for j in range(8):
                            tmp = B_["tmp"][0][:, 0:n]
                            S_.stt("dve", tmp, xb[:, j, :], Avec(l, w, j), rstd, ALU.mult, ALU.mult)
                            S_.ts("dve", HX[:, j, g0:g0 + n], tmp, Bvec(l, w, j), None, ALU.add)

            stage(11 + 100 * l)
            fwd = list(range(NCH))
            bwd = list(range(NCC - 1, -1, -1)) + list(range(NCH - 1, NCC - 1, -1))
            for (order, KW_, G_, ST_d) in ((fwd, KWF, GF, SF_d), (bwd, KWB, GB, SB_d)):
                with AR.scope(), nscope(f"p2_L{l}_{ST_d.name}"):
                    Wk = wload(ws_gran(l, G_RK), (8, 512)); Wv = wload(ws_gran(l, G_RV), (8, 512))
                    ST = AR.alloc("ST", (128, 512), F32); S_.memset("dve", ST, 0.0)
                    tm = AR.alloc("tm", (128, 512), F32)
                    kwb_ = [AR.alloc(f"kw{i}", (128, 512), BF16) for i in range(2)]
                    vb_ = [AR.alloc(f"vb{i}", (128, 512), BF16) for i in range(2)]
                    sb_ = [AR.alloc(f"sb{i}", (128, 512), BF16) for i in range(2)]
                    for ci, c in enumerate(order):
                        kp = ps(); vp = ps()
                        for kt in range(8):
                            S_.mm(kp, HX[:, kt, c * 128:(c + 1) * 128], Wk[:, kt, :], start=(kt == 0), stop=(kt == 7))
                        for kt in range(8):
                            S_.mm(vp, HX[:, kt, c * 128:(c + 1) * 128], Wv[:, kt, :], start=(kt == 0), stop=(kt == 7))
                        kw = kwb_[ci % 2]; vb = vb_[ci % 2]; sb = sb_[ci % 2]
                        S_.tt("dve", kw, kp, KW_, ALU.mult)
                        S_.copy("act", vb, vp)
                        kv = ps()
                        for h in range(4):
                            S_.mm(kv[:, h * 128:(h + 1) * 128], kw[:, h * 128:(h + 1) * 128], vb[:, h * 128:(h + 1) * 128])
                        S_.copy("act", sb, ST)
                        S_.dma("sp", dview(ST_d, ST_d.ap()[c], ((c, c + 1),)), sb)
                        S_.tt("dve", tm, ST, G_, ALU.mult)
                        S_.tt("dve", ST, tm, kv, ALU.add)

            stage(12 + 100 * l)
            with AR.scope(), nscope(f"p2b_L{l}"):
                Wc = wload(ws_gran(l, G_SC), (8, 512)); Wx = wload(ws_gran(l, G_SX), (8, 512))
                Wa = wload(ws_gran(l, G_CA), (8, 512)); Wg = wload(ws_gran(l, G_CG), (8, 512))
                for (g0, n, isc) in blocks:
                    blk = slice(g0, g0 + n)
                    with AR.scope():
                        usc = AR.alloc("usc", (128, 4, n), BF16); ucf = AR.alloc("ucf", (128, 4, n), BF16)
                        xs_ = [AR.alloc(f"xs{i}", (128, n), F32) for i in range(2)]
                        for ch in range(4):
                            pc = proj(Wc, ch * 128, blk, n); px = proj(Wx, ch * 128, blk, n)
                            t = xs_[0]
                            S_.copy("act", t, px[:, 0:n])
                            S_.tt("dve", usc[:, ch, :], pc[:, 0:n], t, ALU.mult)
                            pa = proj(Wa, ch * 128, blk, n); pg = proj(Wg, ch * 128, blk, n)
                            t = xs_[1]
                            S_.act(t, pg[:, 0:n], AF.Sigmoid)
                            S_.tt("dve", ucf[:, ch, :], pa[:, 0:n], t, ALU.mult)
                        p0 = pidx(g0)
                        for U_d, u in ((USC_d, usc), (UCF_d, ucf)):
                            ap = U_d.ap().rearrange("(c p) t -> p c t", p=128)[:, :, p0:p0 + n]
                            S_.dma("sp", dview(U_d, ap, ((0, BW), (p0, p0 + n))), u)

            stage(13 + 100 * l)
            with AR.scope(), nscope(f"p2c_L{l}"):
                Wkv = wload(ws_gran(l, G_AKV), (8, 512))
                for (g0, n, isc) in blocks:
                    blk = slice(g0, g0 + n)
                    with AR.scope():
                        cs = AR.alloc("cs", (128, 2, n), F32); t1 = AR.alloc("t1", (128, n), F32); t2 = AR.alloc("t2", (128, n), F32)
                        src = rope_d.ap().rearrange("a p t -> p a t")[:, :, g0:g0 + n]
                        S_.dma("sp", cs, dview(rope_d, src, ((0, 1),)))
                        pk = proj(Wkv, 0, blk, n); pks = proj(Wkv, 128, blk, n)
                        S_.tt("dve", t1, pk[:, 0:n], cs[:, 0, :], ALU.mult)
                        S_.tt("dve", t2, pks[:, 0:n], cs[:, 1, :], ALU.mult)
                        S_.tt("dve", KR[:, blk], t1, t2, ALU.add)
                        for c in range(g0 // 128, (g0 + n) // 128):
                            pvv = ps()
                            for kt in range(8):
                                S_.mm(pvv[:, 0:128], HX[:, kt, c * 128:(c + 1) * 128], Wkv[:, kt, 256:384], start=(kt == 0), stop=(kt == 7))
                            src3 = pvv[:, 0:128]; src3 = src3.re(src3.ap.rearrange("p (g e) -> p g e", e=64))
                            S_.copy("act", VA[:, c, :, 0:64], src3)

            if l + 1 < DEPTH:
                do_mod(l + 1)
            stage(14 + 100 * l)
            for (g0, n, isc) in blocks:
                if last and isc:
                    continue
                w = 1 if isc else 0
                blk = slice(g0, g0 + n); nchk = n // 128; c0 = g0 // 128
                with AR.scope():
                    BR = AR.alloc("BR", (128, 4, n), BF16)
                    accbox = {}

                    def merge(i):
                        if DEBUG:
                            if "DBG" not in dmem: dmem["DBG"] = Mem("DBG")
                            S_.dma("sp", dview(DBG_d, DBG_d.ap()[i].rearrange("(c p) t -> p c t", p=128)[:, :, g0:g0 + n], ((i, i + 1), (0, BW), (g0, g0 + n))), BR)
                        Wb = wload(dview(WB_d[l], WB_d[l].ap()[i].rearrange("(kt p) n -> p kt n", p=128), ((i, i + 1),)), (4, 1024))
                        with AR.scope(), nscope(f"mrg{i}_L{l}_{g0}"):
                            sg_ = [AR.alloc(f"sg{k}", (128, n), F32) for k in range(2)]
                            tt_ = [AR.alloc(f"tt{k}", (128, n), F32) for k in range(2)]
                            for half in range(2):
                                Wg_ = wload(ws_gran(l, G_GATE + 2 * i + half), (8, 512))
                                for o4 in range(4):
                                    oc = half * 4 + o4
                                    Gp = proj(Wg_, o4 * 128, blk, n)
                                    Pp = ps()
                                    for kt in range(4):
                                        S_.mm(Pp[:, 0:n], Wb[:, kt, oc * 128:(oc + 1) * 128], BR[:, kt, :], start=(kt == 0), stop=(kt == 3))
                                    sg = sg_[oc % 2]; tq = tt_[oc % 2]
                                    S_.act(sg, Gp[:, 0:n], AF.Sigmoid)
                                    if i == 0:
                                        S_.tt("dve", accbox['a'][:, oc, :], Pp[:, 0:n], sg, ALU.mult)
                                    else:
                                        S_.tt("dve", tq, Pp[:, 0:n], sg, ALU.mult)
                                        dst = accbox['m'][:, oc, :] if i == 3 else accbox['a'][:, oc, :]
                                        S_.tt("dve", dst, accbox['a'][:, oc, :], tq, ALU.add)
                                        if i == 3 and l >= 1:
                                            S_.dma("pool", accbox['a'][:, oc, :], xchunk_src(l, oc, g0, n, isc))


                    def tok2feat(on, zs, col0):
                        T = ps(BF16, (128, 4, 128))
                        for k in range(4):
                            S_.transpose(T[:, k, :], on[:, k * 128:(k + 1) * 128], IDB, signal=(k == 3))
                        S_.tt("dve", BR[:, :, col0:col0 + 128], T, zs[:, :, col0:col0 + 128], ALU.mult)

                    with AR.scope(), nscope(f"b3a_L{l}_{g0}"):
                        Wq = wload(ws_gran(l, G_RQ), (8, 512)); Wk = wload(ws_gran(l, G_RK), (8, 512))
                        Wz = wload(ws_gran(l, G_RZ), (8, 512)); Wv = wload(ws_gran(l, G_RV), (8, 512))
                        qT = AR.alloc("qT", (128, 4, n), BF16); kT = AR.alloc("kT", (128, 4, n), BF16)
                        zs = AR.alloc("zs", (128, 4, n), BF16); vt = AR.alloc("vt", (128, nchk, 512), BF16)
                        SFb = AR.alloc("SFb", (128, nchk, 512), BF16); SBb = AR.alloc("SBb", (128, nchk, 512), BF16)
                        for ST_d, dst in ((SF_d, SFb), (SB_d, SBb)):
                            S_.dma("sp", dst, dview(ST_d, ST_d.ap()[c0:c0 + nchk].rearrange("c p f -> p c f"), ((c0, c0 + nchk),)))
                        for h in range(4):
                            P = proj(Wq, h * 128, blk, n); S_.copy("act", qT[:, h, :], P[:, 0:n])
                        for h in range(4):
                            P = proj(Wk, h * 128, blk, n); S_.copy("act", kT[:, h, :], P[:, 0:n])
                        for h in range(4):
                            P = proj(Wz, h * 128, blk, n); S_.act(zs[:, h, :], P[:, 0:n], AF.Silu)
                            S_.ts("dve", zs[:, h, :], zs[:, h, :], par("gn_w", l, slice(h, h + 1)), None, ALU.mult)
                        for ci in range(nchk):
                            P = ps()
                            for kt in range(8):
                                S_.mm(P, HX[:, kt, g0 + ci * 128:g0 + (ci + 1) * 128], Wv[:, kt, :], start=(kt == 0), stop=(kt == 7))
                            S_.copy("act", vt[:, ci, :], P)
                        stage(15 + 100 * l)
                        sTm_ = [AR.alloc(f"sTm{k}", (128, 4, 128), BF16) for k in range(2)]
                        qf_ = [AR.alloc(f"qf{k}", (128, 4, 128), BF16) for k in range(2)]
                        qb_ = [AR.alloc(f"qb{k}", (128, 4, 128), BF16) for k in range(2)]
                        on_ = [AR.alloc(f"on{k}", (128, 512), BF16) for k in range(3)]
                        sqt = AR.alloc("sqt", (128, 4, 128), F32)
                        st_ = [AR.alloc(f"st{k}", (128, 16), F32) for k in range(2)]
                        def stA(ci):
                            cs_ = slice(ci * 128, (ci + 1) * 128)
                            sTm = sTm_[ci % 2]; qf = qf_[ci % 2]; qb = qb_[ci % 2]
                            Ps = ps(F32, (128, 4, 128))
                            for h in range(4):
                                S_.mm(Ps[:, h, :], kT[:, h, cs_], qT[:, h, cs_])
                            S_.tt("dve", sTm, Ps, DT, ALU.mult)
                            S_.tt("dve", qf, qT[:, :, cs_], QWF, ALU.mult)
                            S_.tt("dve", qb, qT[:, :, cs_], QWB, ALU.mult)

                        def stB(ci):
                            sTm = sTm_[ci % 2]; qf = qf_[ci % 2]; qb = qb_[ci % 2]; on = on_[ci % 3]; st = st_[ci % 2]
                            Po = ps(F32, (128, 4, 128))
                            for h in range(4):
                                hs = slice(h * 128, (h + 1) * 128)
                                S_.mm(Po[:, h, :], sTm[:, h, :], vt[:, ci, hs], start=True, stop=False)
                                S_.mm(Po[:, h, :], qf[:, h, :], SFb[:, ci, hs], start=False, stop=False)
                                S_.mm(Po[:, h, :], qb[:, h, :], SBb[:, ci, hs], start=False, stop=True)
                            S_.reduce("dve", st[:, 0:4], Po, ALU.add)
                            S_.act(sqt, Po, AF.Square)
                            S_.reduce("dve", st[:, 4:8], sqt, ALU.add)
                            S_.ts("dve", st[:, 0:4], st[:, 0:4], 1.0 / 128, None, ALU.mult)
                            S_.tt("dve", st[:, 8:12], st[:, 0:4], st[:, 0:4], ALU.mult)
                            S_.stt("dve", st[:, 4:8], st[:, 4:8], 1.0 / 128, st[:, 8:12], ALU.mult, ALU.subtract)
                            S_.act(st[:, 4:8], st[:, 4:8], AF.Sqrt, bias=EPS)
                            S_.recip(st[:, 4:8], st[:, 4:8])
                            for h in range(4):
                                S_.ts("dve", on[:, h * 128:(h + 1) * 128], Po[:, h, :], st[:, h:h + 1], st[:, 4 + h:5 + h], ALU.subtract, ALU.mult)

                        def stC(ci):
                            tok2feat(on_[ci % 3], zs, ci * 128)

                        stA(0)
                        for ci in range(nchk):
                            if ci + 1 < nchk:
                                stA(ci + 1)
                            stB(ci)
                            if ci >= 2:
                                stC(ci - 2)
                        for ci in range(max(nchk - 2, 0), nchk):
                            stC(ci)
                    stage(20 + 100 * l)
                    ACC = AR.alloc("ACC", (128, 8, n), F32); accbox['a'] = ACC
                    MB16 = AR.alloc("MB16", (128, 8, n), BF16); accbox['m'] = MB16
                    merge(0)

                    stage(21 + 100 * l)
                    with AR.scope(), nscope(f"b3b_L{l}_{g0}"):
                        Wz = wload(ws_gran(l, G_SZ), (8, 512)); Wb_ = wload(ws_gran(l, G_SB), (8, 512))
                        cd = wload(dview(CD_d[l], CD_d[l].ap()[4].rearrange("p (k q) -> p k q", q=128), ((4, 5),)), (31, 128))
                        U = AR.alloc("U", (128, 4, n + 2), BF16)
                        p0 = pidx(g0) - 1
                        S_.dma("sp", U, dview(USC_d, USC_d.ap().rearrange("(c p) t -> p c t", p=128)[:, :, p0:p0 + n + 2], ((0, BW), (p0, p0 + n + 2))))
                        bs = AR.alloc("bs", (128, 4, n), F32); zz = AR.alloc("zz", (128, 4, n), BF16)
                        for ch in range(4):
                            Pz = proj(Wz, ch * 128, blk, n)
                            S_.act(zz[:, ch, :], Pz[:, 0:n], AF.Silu)
                        for ch in range(4):
                            Pb = proj(Wb_, ch * 128, blk, n)
                            S_.tt("dve", bs[:, ch, :], Pb[:, 0:n], zz[:, ch, :], ALU.mult)
                        for ch in range(4):
                            Pc = ps()
                            for k in range(3):
                                S_.mm(Pc[:, 0:n], cd[:, ch * 3 + k, :], U[:, ch, k:k + n], start=(k == 0), stop=(k == 2))
                            S_.tt("dve", BR[:, ch, :], Pc[:, 0:n], bs[:, ch, :], ALU.mult)
                    merge(1)

                    stage(22 + 100 * l)
                    with AR.scope(), nscope(f"b3c_L{l}_{g0}"):
                        zz = AR.alloc("zz", (128, 4, n), BF16)
                        uf = AR.alloc("uf", (128, 4, n), F32)
                        mean = AR.alloc("mean", (128, n), F32); rs = AR.alloc("rs", (128, n), F32)
                        with AR.scope():
                            U = AR.alloc("U", (128, 4, n + 30), BF16)
                            ub_ = [AR.alloc(f"ub{k}", (128, n), BF16) for k in range(2)]
                            us_ = [AR.alloc(f"us{k}", (128, n), BF16) for k in range(2)]
                            p0 = pidx(g0) - 15
                            S_.dma("sp", U, dview(UCF_d, UCF_d.ap().rearrange("(c p) t -> p c t", p=128)[:, :, p0:p0 + n + 30], ((0, BW), (p0, p0 + n + 30))))
                            Pm = pv(6); Pq = pv(7)
                            for ch in range(4):
                                cd = wload(dview(CD_d[l], CD_d[l].ap()[ch].rearrange("p (k q) -> p k q", q=128), ((ch, ch + 1),)), (31, 128))
                                Pc = ps()
                                for k in range(31):
                                    S_.mm(Pc[:, 0:n], cd[:, k, :], U[:, ch, k:k + n], start=(k == 0), stop=(k == 30))
                                bcol = par("cfb", l, slice(ch, ch + 1))
                                ub = ub_[ch % 2]; us = us_[ch % 2]
                                S_.act(uf[:, ch, :], Pc[:, 0:n], AF.Identity, bias=bcol)
                                S_.act(us, Pc[:, 0:n], AF.Square, bias=bcol)
                                S_.copy("dve", ub, uf[:, ch, :])
                                S_.mm(Pm[:, 0:n], ONESC, ub, start=(ch == 0), stop=(ch == 3))
                                S_.mm(Pq[:, 0:n], ONESC, us, start=(ch == 0), stop=(ch == 3))
                            Wz = wload(ws_gran(l, G_CZ), (8, 512))
                            for ch in range(4):
                                Pz = proj(Wz, ch * 128, blk, n)
                                S_.act(zz[:, ch, :], Pz[:, 0:n], AF.Silu)
                            S_.copy("act", mean, Pm[:, 0:n])
                            S_.act(rs, Pm[:, 0:n], AF.Square)
                            S_.tt("dve", rs, Pq[:, 0:n], rs, ALU.subtract)
                            S_.act(rs, rs, AF.Sqrt, bias=EPS)
                            S_.recip(rs, rs)
                        t1_ = [AR.alloc(f"t1{k}", (128, n), F32) for k in range(2)]
                        for ch in range(4):
                            t1 = t1_[ch % 2]
                            S_.tt("dve", t1, uf[:, ch, :], mean, ALU.subtract)
                            S_.tt("dve", t1, t1, rs, ALU.mult)
                            S_.act(t1, t1, AF.Silu, bias=par("lnb", l, slice(ch, ch + 1)), scale=par("lnw", l, slice(ch, ch + 1)))
                            S_.tt("dve", BR[:, ch, :], t1, zz[:, ch, :], ALU.mult)
                    merge(2)

                    stage(23 + 100 * l)
                    with AR.scope(), nscope(f"b3d_L{l}_{g0}"):
                        Wq = wload(ws_gran(l, G_AQ), (8, 512)); Wqs = wload(ws_gran(l, G_AQS), (8, 512)); Wz = wload(ws_gran(l, G_AZ), (8, 512))
                        cs = AR.alloc("cs", (128, 2, n), F32); t1 = AR.alloc("t1", (128, 4, n), F32); t2 = AR.alloc("t2", (128, n), F32)
                        zs = AR.alloc("zs", (128, 4, n), BF16)
                        S_.dma("sp", cs, dview(rope_d, rope_d.ap().rearrange("a p t -> p a t")[:, :, g0:g0 + n], ((0, 1),)))
                        for m in range(4):
                            Pq_ = proj(Wq, m * 128, blk, n)
                            S_.tt("dve", t1[:, m, :], Pq_[:, 0:n], cs[:, 0, :], ALU.mult)
                        for m in range(4):
                            Pqs = proj(Wqs, m * 128, blk, n)
                            S_.tt("dve", t2, Pqs[:, 0:n], cs[:, 1, :], ALU.mult)
                            S_.tt("dve", QA[0:64, m, 0:n], t1[0:64, m, :], t2[0:64, :], ALU.add)
                            S_.tt("dve", QB[64:128, m, 0:n], t1[64:128, m, :], t2[64:128, :], ALU.add)
                        for m in range(4):
                            Pz = proj(Wz, m * 128, blk, n)
                            S_.act(zs[:, m, :], Pz[:, 0:n], AF.Silu)
                        pt_ = [AR.alloc(f"pt{k}", (128, 4, 128), BF16) for k in range(4)]
                        on_ = [AR.alloc(f"on{k}", (128, 8, 64), BF16) for k in range(2)]
                        dn_ = [AR.alloc(f"dn{k}", (128, 8), F32) for k in range(2)]
                        pti = 0
                        for qi in range(nchk):
                            qc = c0 + qi; qs_ = slice(qi * 128, (qi + 1) * 128)
                            if isc:
                                kbs = [(c, None) for c in range(NCC)]
                            else:
                                kbs = []
                                if qc - 1 >= NCC: kbs.append((qc - 1, MPREV))
                                kbs.append((qc, None))
                                if qc + 1 < NCH: kbs.append((qc + 1, MNEXT))
                                kbs += [(c, None) for c in range(NCC)]
                            on = on_[qi % 2]; dn = dn_[qi % 2]
                            Og = [pv(6, F32, (128, 4, 65)), pv(7, F32, (128, 4, 65))]
                            steps = [(bi, kb, msk, g) for bi, (kb, msk) in enumerate(kbs) for g in range(2)]
                            QQ = (QA, QB)

                            def score(step):
                                bi, kb, msk, g = step
                                Pst = ps(F32, (128, 4, 128))
                                S_.mm(Pst, KR[:, kb * 128:(kb + 1) * 128], QQ[g][:, :, qs_])
                                return Pst
                            PD = 3
                            pend = [score(st_) for st_ in steps[:PD]]
                            for si, step in enumerate(steps):
                                bi, kb, msk, g = step
                                Pst = pend.pop(0)
                                if si + PD < len(steps):
                                    pend.append(score(steps[si + PD]))
                                pt = pt_[pti % 4]; pti += 1
                                S_.act(pt, Pst, AF.Exp, scale=ATT_SCALE)
                                if msk is not None:
                                    mb = msk.re(msk.ap.unsqueeze(1).to_broadcast([128, 4, 128]))
                                    S_.tt("dve", pt, pt, mb, ALU.mult)
                                for hl in range(4):
                                    S_.mm(Og[g][:, hl, :], pt[:, hl, :], VA[:, kb, g, :], start=(bi == 0 and hl == 0), stop=(bi == len(kbs) - 1),
                                          signal=(bi == len(kbs) - 1 and hl == 3))
                            for g in range(2):
                                O = Og[g]
                                es_ = ESINK[:, g * 4:(g + 1) * 4]
                                S_.tt("dve", dn[:, g * 4:(g + 1) * 4], O[:, :, 64], es_, ALU.add)
                                S_.recip(dn[:, g * 4:(g + 1) * 4], dn[:, g * 4:(g + 1) * 4])
                                dsl = dn[:, g * 4:(g + 1) * 4]
                                rb = dsl.re(dsl.ap.unsqueeze(2).to_broadcast([128, 4, 64]))
                                S_.tt("dve", on[:, g * 4:(g + 1) * 4, :], O[:, :, 0:64], rb, ALU.mult)
                            onf = on.re(on.ap.rearrange("p h e -> p (h e)"))
                            T = ps(BF16, (128, 4, 128))
                            for k in range(4):
                                S_.transpose(T[:, k, :], onf[:, k * 128:(k + 1) * 128], IDB, signal=(k == 3))
                            S_.tt("dve", BR[:, :, qs_], T, zs[:, :, qs_], ALU.mult)
                    merge(3)

                    stage(24 + 100 * l)
                    with AR.scope(), nscope(f"b3e_L{l}_{g0}"):
                        XN = ACC
                        if last:
                            sq = AR.alloc("sq", (128, 8, n), BF16)
                        Wo_ = [wload(dview(WO_d[l], WO_d[l].ap().rearrange("(kt p) n -> p kt n", p=128)[:, :, half * 512:(half + 1) * 512], ((0, D), (half * 512, half * 512 + 512))), (8, 512)) for half in range(2)]
                        if l == 0:
                            for oc in range(8):
                                S_.dma("sp", XN[:, oc, :], xchunk_src(l, oc, g0, n, isc))
                        for half in range(2):
                            Wo = Wo_[half]
                            for o4 in range(4):
                                oc = half * 4 + o4
                                Y = ps()
                                for kt in range(8):
                                    S_.mm(Y[:, 0:n], Wo[:, kt, o4 * 128:(o4 + 1) * 128], MB16[:, kt, :], start=(kt == 0), stop=(kt == 7))
                                S_.stt("dve", XN[:, oc, :], Y[:, 0:n], Gvec(l, w, oc), XN[:, oc, :], ALU.mult, ALU.add)
                                if not last:
                                    S_.dma(STQ, dview(nxt, nxt.ap()[oc * 128:(oc + 1) * 128, g0:g0 + n], ((oc * 128, oc * 128 + 128), (g0, g0 + n))), XN[:, oc, :])
                        if last:
                            S_.act(sq, XN, AF.Square)
                            P = ps()
                            for j in range(8):
                                S_.mm(P[:, 0:n], ONESD, sq[:, j, :], start=(j == 0), stop=(j == 7))
                            rstd = AR.alloc("rstd", (128, n), F32)
                            S_.act(rstd, P[:, 0:n], AF.Sqrt, bias=EPS)
                            S_.recip(rstd, rstd)
                            for j in range(8):
                                S_.stt("dve", XN[:, j, :], XN[:, j, :], par("fnw", slice(j, j + 1)), rstd, ALU.mult, ALU.mult)
                            t0_ = g0 - L
                            S_.dma("sp", dview(yT_d, yT_d.ap().rearrange("(j p) t -> p j t", p=128)[:, :, t0_:t0_ + n], ((0, D), (t0_, t0_ + n))), XN)
        S_.wait_all("sp", [dmem["yT"]])
        print(f"[build] ops={S_.nops} waits={S_.nwaits} sems={S_.nsem} arena_top={AR.top}")
    return nc


def make_in_maps(inp, S, L, DEPTH, nb):
    C, R = host_consts(S, L)
    maps = []
    for b in range(nb):
        maps.append({
            "xT": np.ascontiguousarray(inp["x"][b].T), "cT": np.ascontiguousarray(inp["ctx"][b].T),
            "par": host_params(inp, b, DEPTH), "con": C, "rope": R,
            "w_mod": inp["w_mod"][:DEPTH], "w_in": inp["w_in"][:DEPTH], "w_branch": inp["w_branch"][:DEPTH], "w_out": inp["w_out"][:DEPTH],
        })
    return maps


def kernel(**inputs):
    from concourse.bass_utils import run_bass_kernel_spmd
    inp = {k: np.asarray(v) for k, v in inputs.items()}
    B, S, _ = inp["x"].shape; L = inp["ctx"].shape[1]; DEPTH = inp["w_in"].shape[0]
    nc = build_program(S, L, DEPTH)
    maps = make_in_maps(inp, S, L, DEPTH, B)
    res = run_bass_kernel_spmd(nc, maps, core_ids=list(range(B)))
    out = np.stack([np.ascontiguousarray(r["yT"].T) for r in res.results], 0)
    return out.astype(np.float32)
```
